# Optimizing a Trainium2 kernel written in Bass

```python
import jax, jax.numpy as jnp
from jax import lax
import numpy as np

D_MODEL = 2048
BATCH = 8
SEQ = 2048
DEPTH = 1
DEC_BATCH = 4
DEC_SEQ = 4096
PAST_LEN = 128

MIX_W = D_MODEL
RWKV_W = MIX_W // 2
RWKV_HEAD = 64
RWKV_HEADS = RWKV_W // RWKV_HEAD
DECAY_LORA = 64
ICLR_LORA = 64
GATE_LORA = 128
SSM_W = MIX_W - RWKV_W
SSM_HEAD = 64
SSM_HEADS = SSM_W // SSM_HEAD
SSM_GROUPS = 2
SSM_STATE = 128
SSM_CONV = 7
SSM_CHUNK = 128
SSM_CONV_CH = SSM_W + 2 * SSM_GROUPS * SSM_STATE
FFN_CONV = 3
D_FF = 5632
N_DIR = 2
N_MOD = 6
NORM_EPS = 1e-6
GN_EPS = 64e-5
GATED_EPS = 1e-5

WD0 = 3 * RWKV_W
AD0 = WD0 + N_DIR * DECAY_LORA
GD0 = AD0 + N_DIR * ICLR_LORA
RWKV_COLS = GD0 + GATE_LORA
SSM_COLS = SSM_W + SSM_CONV_CH + N_DIR * SSM_HEADS
P_IN = RWKV_COLS + SSM_COLS

kernel_name = "hybrid_bidir_rwkv7_mamba2_adaln_encoder"


def rmsnorm(x, w, eps=NORM_EPS):
    xf = x.astype(jnp.float32)
    y = xf * lax.rsqrt(jnp.mean(jnp.square(xf), axis=-1, keepdims=True) + eps)
    return (y * w.astype(jnp.float32)).astype(x.dtype)


def dwconv_centred(x, w, b):
    k_w = w.shape[0]
    t = x.shape[1]
    left = (k_w - 1) // 2
    xp = jnp.pad(x, ((0, 0), (left, k_w - 1 - left), (0, 0)))
    y = b
    for i in range(k_w):
        y = y + xp[:, i:i + t] * w[i]
    return y


def token_shift_centred(p, mu):
    prev = jnp.pad(p[:, :-1], ((0, 0), (1, 0), (0, 0)))
    nxt = jnp.pad(p[:, 1:], ((0, 0), (0, 1), (0, 0)))
    return p + mu[0] * (prev - p) + mu[1] * (nxt - p)


def rwkv7_step(state, inp):
    r_t, w_t, k_t, v_t, a_t, b_t = inp
    sa = jnp.einsum("bdhij,bdhj->bdhi", state, a_t)
    state = state * w_t[..., None, :] + sa[..., :, None] * b_t[..., None, :] + v_t[..., :, None] * k_t[..., None, :]
    y = jnp.einsum("bdhij,bdhj->bdhi", state, r_t)
    return state, y


def rwkv7_mixer(p, mu, w0, w2, a0, a2, g2, k_k, k_a, r_k, lnx_w, lnx_b):
    f32 = jnp.float32
    b, t, _ = p.shape
    h_n, n = RWKV_HEADS, RWKV_HEAD
    xs = token_shift_centred(p, mu)
    r = xs[..., :RWKV_W].astype(f32)
    k = xs[..., RWKV_W:2 * RWKV_W].astype(f32)
    v = xs[..., 2 * RWKV_W:3 * RWKV_W].astype(f32)
    wd = xs[..., WD0:AD0].reshape(b, t, N_DIR, DECAY_LORA)
    ad = xs[..., AD0:GD0].reshape(b, t, N_DIR, ICLR_LORA)
    gd = xs[..., GD0:]
    w_log = (w0 + jnp.einsum("btdr,drc->btdc", jnp.tanh(wd), w2)).astype(f32)
    decay = jnp.exp(-jnp.exp(-jax.nn.softplus(-w_log) - 0.5))
    a = jax.nn.sigmoid((a0 + jnp.einsum("btdr,drc->btdc", ad, a2)).astype(f32))
    g = (jax.nn.sigmoid(gd) @ g2).astype(f32)
    kk = (k * k_k.astype(f32)).reshape(b, t, h_n, n)
    kk = (kk / jnp.maximum(jnp.linalg.norm(kk, axis=-1, keepdims=True), 1e-12)).reshape(b, t, RWKV_W)
    k_dir = k[:, :, None] * (1.0 + (a - 1.0) * k_a.astype(f32))

    def shared(z):
        return jnp.stack([z, z], axis=2)

    def per_dir(z):
        z = jnp.stack([z[:, :, 0], jnp.flip(z[:, :, 1], axis=1)], axis=2)
        return z.reshape(b, t, N_DIR, h_n, n).transpose(1, 0, 2, 3, 4)

    seqs = (per_dir(shared(r)), per_dir(decay), per_dir(k_dir), per_dir(shared(v)),
            per_dir(shared(-kk)), per_dir(shared(kk) * a))
    s0 = jnp.zeros((b, N_DIR, h_n, n, n), f32)
    _, y = lax.scan(rwkv7_step, s0, seqs)
    y = y.transpose(1, 0, 2, 3, 4)
    y = y[:, :, 0] + jnp.flip(y[:, :, 1], axis=1)
    mean = jnp.mean(y, axis=-1, keepdims=True)
    var = jnp.mean(jnp.square(y - mean), axis=-1, keepdims=True)
    y = ((y - mean) * lax.rsqrt(var + GN_EPS)).reshape(b, t, RWKV_W) * lnx_w.astype(f32) + lnx_b.astype(f32)
    bonus = jnp.sum(r.reshape(b, t, h_n, n) * k_dir.sum(axis=2).reshape(b, t, h_n, n) * r_k.astype(f32),
                    axis=-1, keepdims=True) * v.reshape(b, t, h_n, n)
    return ((y + bonus.reshape(b, t, RWKV_W)) * g).astype(p.dtype)


def segsum(z):
    t = z.shape[-1]
    zc = jnp.cumsum(z, axis=-1)
    diff = zc[..., :, None] - zc[..., None, :]
    return jnp.where(jnp.tril(jnp.ones((t, t), dtype=bool)), diff, -jnp.inf)


def ssd_chunked(xs, dt, a_h, bm, cm):
    b, t, h, hp = xs.shape
    g, n = bm.shape[2], bm.shape[3]
    j = h // g
    q = SSM_CHUNK
    c = t // q
    xd = (xs * dt[..., None]).reshape(b, c, q, g, j, hp)
    bc = bm.reshape(b, c, q, g, n)
    cc = cm.reshape(b, c, q, g, n)
    adt = (dt * a_h).reshape(b, c, q, g, j).transpose(0, 1, 3, 4, 2)
    acs = jnp.cumsum(adt, axis=-1)
    lmat = jnp.exp(segsum(adt))
    cb = jnp.einsum("bclgn,bcsgn->bcgls", cc, bc)
    y_diag = jnp.einsum("bcgls,bcgjls,bcsgjp->bclgjp", cb, lmat, xd)
    decay_in = jnp.exp(acs[..., -1:] - acs)
    states = jnp.einsum("bcsgn,bcgjs,bcsgjp->bcgjpn", bc, decay_in, xd)
    chunk_a = jnp.pad(acs[..., -1], ((0, 0), (1, 0), (0, 0), (0, 0)))
    decay_chunk = jnp.exp(segsum(chunk_a.transpose(0, 2, 3, 1)))
    states = jnp.concatenate([jnp.zeros_like(states[:, :1]), states], axis=1)
    states_in = jnp.einsum("bgjzc,bcgjpn->bzgjpn", decay_chunk, states)[:, :-1]
    y_off = jnp.einsum("bclgn,bcgjpn,bcgjl->bclgjp", cc, states_in, jnp.exp(acs))
    return (y_diag + y_off).reshape(b, t, h, hp)


def mamba2_mixer(p, conv_w, conv_b, dt_bias, a_log, d_skip, norm_w):
    f32 = jnp.float32
    b, t, _ = p.shape
    gn = SSM_GROUPS * SSM_STATE
    z = p[..., :SSM_W]
    xbc = jax.nn.silu(dwconv_centred(p[..., SSM_W:SSM_W + SSM_CONV_CH], conv_w, conv_b))
    dt_raw = p[..., SSM_W + SSM_CONV_CH:].reshape(b, t, N_DIR, SSM_HEADS)
    xs = xbc[..., :SSM_W].reshape(b, t, SSM_HEADS, SSM_HEAD).astype(f32)
    bm = xbc[..., SSM_W:SSM_W + gn].reshape(b, t, SSM_GROUPS, SSM_STATE).astype(f32)
    cm = xbc[..., SSM_W + gn:].reshape(b, t, SSM_GROUPS, SSM_STATE).astype(f32)
    dt = jax.nn.softplus(dt_raw.astype(f32) + dt_bias.astype(f32))
    a_h = -jnp.exp(a_log.astype(f32))
    y_f = ssd_chunked(xs, dt[:, :, 0], a_h[0], bm, cm)
    fl = lambda u: jnp.flip(u, axis=1)
    y_b = fl(ssd_chunked(fl(xs), fl(dt[:, :, 1]), a_h[1], fl(bm), fl(cm)))
    y = y_f + y_b + xs * d_skip.astype(f32)[:, None]
    y = y.reshape(b, t, SSM_W) * jax.nn.silu(z.astype(f32))
    y = y * lax.rsqrt(jnp.mean(jnp.square(y), axis=-1, keepdims=True) + GATED_EPS) * norm_w.astype(f32)
    return y.astype(p.dtype)


def conv_glu_ffn(h, w_up, conv_w, conv_b, w_down):
    u = dwconv_centred(h @ w_up, conv_w, conv_b)
    return (jax.nn.silu(u[..., :D_FF]) * u[..., D_FF:]) @ w_down


def encoder(x, c, p):
    sc = jax.nn.silu(c)
    for l in range(DEPTH):
        mod = sc @ p["w_ada"][l] + p["b_ada"][l]
        sh1, sc1, g1, sh2, sc2, g2 = jnp.split(mod[:, None, :], N_MOD, axis=-1)
        h = rmsnorm(x, p["norm1_w"][l]) * (1.0 + sc1) + sh1
        proj = h @ p["w_in"][l]
        y_rw = rwkv7_mixer(proj[..., :RWKV_COLS], p["rwkv_mu"][l], p["rwkv_w0"][l], p["rwkv_w2"][l],
                           p["rwkv_a0"][l], p["rwkv_a2"][l], p["rwkv_g2"][l], p["rwkv_k_k"][l],
                           p["rwkv_k_a"][l], p["rwkv_r_k"][l], p["rwkv_lnx_w"][l], p["rwkv_lnx_b"][l])
        y_ssm = mamba2_mixer(proj[..., RWKV_COLS:], p["ssm_conv_w"][l], p["ssm_conv_b"][l],
                             p["ssm_dt_bias"][l], p["ssm_a_log"][l], p["ssm_d"][l], p["ssm_norm_w"][l])
        x = x + g1 * (jnp.concatenate([y_rw, y_ssm], axis=-1) @ p["w_out"][l])
        h = rmsnorm(x, p["norm2_w"][l]) * (1.0 + sc2) + sh2
        x = x + g2 * conv_glu_ffn(h, p["ffn_w_up"][l], p["ffn_conv_w"][l], p["ffn_conv_b"][l], p["ffn_w_down"][l])
    modf = sc @ p["w_ada_final"] + p["b_ada_final"]
    shf, scf = jnp.split(modf[:, None, :], 2, axis=-1)
    return rmsnorm(x, p["final_norm_w"]) * (1.0 + scf) + shf


def setup_inputs(seed: int = 0) -> dict:
    key = jax.random.key(seed)
    ks = iter(jax.random.split(key, 48))
    f32 = jnp.float32
    nrm = lambda shape, s: jax.random.normal(next(ks), shape, f32) * s
    uni = lambda shape, lo, hi: jax.random.uniform(next(ks), shape, f32, lo, hi)
    gain = lambda shape: 1.0 + nrm(shape, 0.02)
    dt0 = jnp.exp(uni((DEPTH, N_DIR, SSM_HEADS), float(np.log(1e-3)), float(np.log(1e-1))))
    return {
        "x_prompt": nrm((BATCH, SEQ, D_MODEL), 1.0),
        "x_sample": nrm((DEC_BATCH, DEC_SEQ, D_MODEL), 1.0),
        "c_prompt": nrm((BATCH, D_MODEL), 1.0),
        "c_sample": nrm((DEC_BATCH, D_MODEL), 1.0),
        "norm1_w": gain((DEPTH, D_MODEL)),
        "w_in": nrm((DEPTH, D_MODEL, P_IN), D_MODEL ** -0.5),
        "rwkv_mu": uni((DEPTH, 2, RWKV_COLS), 0.0, 0.5),
        "rwkv_w0": uni((DEPTH, N_DIR, RWKV_W), -6.0, 1.0),
        "rwkv_w2": nrm((DEPTH, N_DIR, DECAY_LORA, RWKV_W), 0.5 * DECAY_LORA ** -0.5),
        "rwkv_a0": nrm((DEPTH, N_DIR, RWKV_W), 0.5),
        "rwkv_a2": nrm((DEPTH, N_DIR, ICLR_LORA, RWKV_W), 0.5 * ICLR_LORA ** -0.5),
        "rwkv_g2": nrm((DEPTH, GATE_LORA, RWKV_W), GATE_LORA ** -0.5),
        "rwkv_k_k": 0.85 + nrm((DEPTH, RWKV_W), 0.02),
        "rwkv_k_a": gain((DEPTH, RWKV_W)),
        "rwkv_r_k": nrm((DEPTH, RWKV_HEADS, RWKV_HEAD), 0.1),
        "rwkv_lnx_w": gain((DEPTH, RWKV_W)),
        "rwkv_lnx_b": nrm((DEPTH, RWKV_W), 0.01),
        "ssm_conv_w": nrm((DEPTH, SSM_CONV, SSM_CONV_CH), SSM_CONV ** -0.5),
        "ssm_conv_b": nrm((DEPTH, SSM_CONV_CH), 0.01),
        "ssm_dt_bias": dt0 + jnp.log(-jnp.expm1(-dt0)),
        "ssm_a_log": jnp.log(uni((DEPTH, N_DIR, SSM_HEADS), 1.0, 16.0)),
        "ssm_d": gain((DEPTH, SSM_HEADS)),
        "ssm_norm_w": gain((DEPTH, SSM_W)),
        "w_out": nrm((DEPTH, MIX_W, D_MODEL), MIX_W ** -0.5),
        "norm2_w": gain((DEPTH, D_MODEL)),
        "ffn_w_up": nrm((DEPTH, D_MODEL, 2 * D_FF), D_MODEL ** -0.5),
        "ffn_conv_w": nrm((DEPTH, FFN_CONV, 2 * D_FF), FFN_CONV ** -0.5),
        "ffn_conv_b": nrm((DEPTH, 2 * D_FF), 0.01),
        "ffn_w_down": nrm((DEPTH, D_FF, D_MODEL), D_FF ** -0.5),
        "w_ada": nrm((DEPTH, D_MODEL, N_MOD * D_MODEL), 0.5 * D_MODEL ** -0.5),
        "b_ada": nrm((DEPTH, N_MOD * D_MODEL), 0.01),
        "final_norm_w": gain((D_MODEL,)),
        "w_ada_final": nrm((D_MODEL, 2 * D_MODEL), 0.5 * D_MODEL ** -0.5),
        "b_ada_final": nrm((2 * D_MODEL,), 0.01),
    }


def reference(x_prompt, x_sample, c_prompt, c_sample, norm1_w, w_in, rwkv_mu, rwkv_w0, rwkv_w2, rwkv_a0,
              rwkv_a2, rwkv_g2, rwkv_k_k, rwkv_k_a, rwkv_r_k, rwkv_lnx_w, rwkv_lnx_b, ssm_conv_w, ssm_conv_b,
              ssm_dt_bias, ssm_a_log, ssm_d, ssm_norm_w, w_out, norm2_w, ffn_w_up, ffn_conv_w, ffn_conv_b,
              ffn_w_down, w_ada, b_ada, final_norm_w, w_ada_final, b_ada_final):
    p = {
        "norm1_w": norm1_w, "w_in": w_in, "rwkv_mu": rwkv_mu, "rwkv_w0": rwkv_w0, "rwkv_w2": rwkv_w2,
        "rwkv_a0": rwkv_a0, "rwkv_a2": rwkv_a2, "rwkv_g2": rwkv_g2, "rwkv_k_k": rwkv_k_k,
        "rwkv_k_a": rwkv_k_a, "rwkv_r_k": rwkv_r_k, "rwkv_lnx_w": rwkv_lnx_w, "rwkv_lnx_b": rwkv_lnx_b,
        "ssm_conv_w": ssm_conv_w, "ssm_conv_b": ssm_conv_b, "ssm_dt_bias": ssm_dt_bias,
        "ssm_a_log": ssm_a_log, "ssm_d": ssm_d, "ssm_norm_w": ssm_norm_w, "w_out": w_out,
        "norm2_w": norm2_w, "ffn_w_up": ffn_w_up, "ffn_conv_w": ffn_conv_w, "ffn_conv_b": ffn_conv_b,
        "ffn_w_down": ffn_w_down, "w_ada": w_ada, "b_ada": b_ada, "final_norm_w": final_norm_w,
        "w_ada_final": w_ada_final, "b_ada_final": b_ada_final,
    }
    y_prompt = encoder(x_prompt, c_prompt, p)
    y_sample = encoder(x_sample, c_sample, p)
    return (y_prompt, y_sample)
```

```python
import contextlib
import numpy as np
import concourse.bass as bass
import concourse.mybir as mybir
from concourse.bass_utils import run_bass_kernel_spmd

F32 = mybir.dt.float32
BF16 = mybir.dt.bfloat16
AF = mybir.ActivationFunctionType
ALU = mybir.AluOpType
AX = mybir.AxisListType

ENGS = ("pe", "act", "dve", "pool", "sp")
NSLOT = 6
D = 2048
KC = 16
PIN = 6048
DFF = 5632
ROW_R, ROW_K, ROW_V, ROW_WD, ROW_AD, ROW_GD, ROW_Z, ROW_XS, ROW_B, ROW_C, ROW_DT = (
    0, 1024, 2048, 3072, 3200, 3328, 3456, 4480, 5504, 5760, 6016)
DEC_C = float(np.exp(-0.5))


class Buf:
    __slots__ = ("name", "ws", "rs", "multi")

    def __init__(self, name, multi=False):
        self.name = name
        self.ws = []
        self.rs = []
        self.multi = multi


class T:
    def __init__(self, ap, b):
        self.ap = ap
        self.b = b

    def __getitem__(self, k):
        return self.ap[k]


class Prog:
    def __init__(self, nc):
        self.nc = nc
        self.ops = []
        self.streams = {e: [] for e in ENGS}
        self.bufs = []

    def buf(self, name, multi=False):
        b = Buf(name, multi)
        self.bufs.append(b)
        return b

    def op(self, eng, fn, reads=(), writes=(), dma=False):
        oid = len(self.ops)
        deps = {}
        for b in reads:
            for w in b.ws:
                deps[w] = "raw"
        for b in writes:
            for r in b.rs:
                deps.setdefault(r, "war")
            if not b.multi:
                for w in b.ws:
                    deps.setdefault(w, "waw")
        self.ops.append(dict(eng=eng, fn=fn, deps=deps, dma=dma, sig=False))
        self.streams[eng].append(oid)
        for b in reads:
            b.rs.append(oid)
        for b in writes:
            if b.rs or not b.multi:
                b.ws = [oid]
                b.rs = []
            else:
                b.ws.append(oid)
        return oid

    def barrier(self):
        front = []
        for e in ENGS:
            lastc = None
            dm = []
            for oid in reversed(self.streams[e]):
                o = self.ops[oid]
                if o["fn"] is None:
                    continue
                if o["dma"]:
                    if len(dm) < NSLOT:
                        dm.append(oid)
                elif lastc is None:
                    lastc = oid
                if lastc is not None and len(dm) >= NSLOT:
                    break
            if lastc is not None:
                front.append(lastc)
            front += dm
        for e in ENGS:
            oid = len(self.ops)
            self.ops.append(dict(eng=e, fn=None, deps={f: "raw" for f in front}, dma=False, sig=False))
            self.streams[e].append(oid)
        for b in self.bufs:
            b.ws = []
            b.rs = []

    def run(self, engsems, dmasems, block):
        ops = self.ops
        for o in ops:
            e = o["eng"]
            nd = set()
            for d, kind in o["deps"].items():
                od = ops[d]
                if (not o["dma"]) and (not od["dma"]) and od["eng"] == e and o["fn"] is not None:
                    if e == "pe" or (kind != "raw" and e != "pool"):
                        continue
                nd.add(d)
            o["deps"] = nd
            for d in nd:
                ops[d]["sig"] = True
        cnt = {e: 0 for e in ENGS}
        dcnt = {}
        dslot = {e: 0 for e in ENGS}
        for e in ENGS:
            for oid in self.streams[e]:
                o = ops[oid]
                if o["fn"] is None:
                    continue
                if o["dma"]:
                    s = dslot[e] % NSLOT
                    dslot[e] += 1
                    prev = dcnt.get((e, s), 0)
                    dcnt[(e, s)] = prev + 16
                    o["sem"] = ("dma", e, s)
                    o["val"] = prev + 16
                    o["prev"] = prev
                elif o["sig"]:
                    cnt[e] += 1
                    o["sem"] = ("eng", e)
                    o["val"] = cnt[e]

        def semh(k):
            return engsems[k[1]] if k[0] == "eng" else dmasems[(k[1], k[2])]

        def make(e):
            def body(eng):
                known = {}
                for oid in self.streams[e]:
                    o = ops[oid]
                    need = {}
                    for d in o["deps"]:
                        od = ops[d]
                        k = od["sem"]
                        if od["val"] > need.get(k, 0):
                            need[k] = od["val"]
                    if o["dma"] and o["prev"] > 0:
                        k = o["sem"]
                        if o["prev"] > need.get(k, 0):
                            need[k] = o["prev"]
                    for k, v in need.items():
                        if known.get(k, 0) >= v:
                            continue
                        known[k] = v
                        eng.wait_ge(semh(k), v)
                    if o["fn"] is None:
                        continue
                    ins = o["fn"](eng)
                    if o["dma"]:
                        ins.then_inc(semh(o["sem"]), 16)
                    elif o["sig"]:
                        ins.then_inc(semh(o["sem"]), 1)
            return body

        names = {"pe": "tensor", "act": "scalar", "dve": "vector", "pool": "gpsimd", "sp": "sync"}
        for e in ENGS:
            if self.streams[e]:
                getattr(block, names[e])(make(e))


class KB:
    def __init__(self, nc, TT, es):
        self.nc = nc
        self.TT = TT
        self.HT = TT // 2
        self.es = es
        self.p = Prog(nc)
        self.psn = 0
        self.aoff = 0

    def sb(self, name, shape, dt):
        t = self.es.enter_context(self.nc.sbuf_tensor(name, list(shape), dt))
        return T(t[:] if len(shape) == 2 else t[tuple(slice(None) for _ in shape)], self.p.buf(name))

    def dram(self, name, shape, dt, multi=True):
        t = self.nc.dram_tensor(name, list(shape), dt, kind=("ExternalOutput" if self.dbg else "Internal")).ap()
        return T(t, self.p.buf(name, multi=multi))

    def init_arena(self, words):
        self.arena = self.es.enter_context(self.nc.sbuf_tensor("arena", [128, words], F32))
        self.awords = words
        self.pst = []
        for i in range(8):
            t = self.es.enter_context(self.nc.psum_tensor(f"psb{i}", [128, 512], F32))
            self.pst.append(T(t[:, :], self.p.buf(f"psb{i}")))

    def reset(self):
        self.p.barrier()
        self.aoff = 0

    def carve(self, name, shape, dt, parts=128):
        n = int(np.prod(shape[1:]))
        words = n if dt == F32 else (n + 1) // 2
        a = self.arena[0:shape[0], self.aoff:self.aoff + words]
        self.aoff += words
        assert self.aoff <= self.awords, (name, self.aoff, self.awords)
        if dt != F32:
            a = a.bitcast(dt)
            if n % 2:
                a = a[:, 0:n]
        if len(shape) == 3:
            a = a.rearrange("p (a b) -> p a b", a=shape[1])
        elif len(shape) == 4:
            a = a.rearrange("p (a b c) -> p a b c", a=shape[1], b=shape[2])
        return T(a, self.p.buf(name))

    def ps(self):
        t = self.pst[self.psn % 8]
        self.psn += 1
        return t

    def mm(self, out, lhsT, rhs, start=True, stop=True, R=(), W=()):
        return self.p.op("pe", lambda e: e.matmul(out, lhsT=lhsT, rhs=rhs, start=start, stop=stop),
                         reads=[t.b for t in R], writes=[t.b for t in W])

    def tr(self, out, in_, ident, R=(), W=()):
        return self.p.op("pe", lambda e: e.transpose(out, in_, ident), reads=[t.b for t in R], writes=[t.b for t in W])

    def act(self, out, in_, func, bias=0.0, scale=1.0, accum=None, R=(), W=()):
        if accum is None:
            f = lambda e: e.activation(out=out, in_=in_, func=func, bias=bias, scale=scale)
        else:
            f = lambda e: e.activation(out=out, in_=in_, func=func, bias=bias, scale=scale, accum_out=accum)
        return self.p.op("act", f, reads=[t.b for t in R], writes=[t.b for t in W])

    def tt(self, eng, out, a, b, op, R=(), W=()):
        return self.p.op(eng, lambda e: e.tensor_tensor(out=out, in0=a, in1=b, op=op),
                         reads=[t.b for t in R], writes=[t.b for t in W])

    def ts(self, eng, out, a, s1, s2, op0, op1=None, R=(), W=()):
        if op1 is None:
            f = lambda e: e.tensor_scalar(out=out, in0=a, scalar1=s1, scalar2=None, op0=op0)
        else:
            f = lambda e: e.tensor_scalar(out=out, in0=a, scalar1=s1, scalar2=s2, op0=op0, op1=op1)
        return self.p.op(eng, f, reads=[t.b for t in R], writes=[t.b for t in W])

    def stt(self, out, in0, scalar, in1, op0, op1, R=(), W=()):
        return self.p.op("dve", lambda e: e.scalar_tensor_tensor(out=out, in0=in0, scalar=scalar, in1=in1, op0=op0, op1=op1),
                         reads=[t.b for t in R], writes=[t.b for t in W])

    def rsqrt(self, out, in_, scale, eps, R=(), W=()):
        self.act(out, in_, AF.Sqrt, bias=eps, scale=scale, R=R, W=W)
        return self.p.op("dve", lambda e: e.reciprocal(out=out, in_=out), reads=[t.b for t in W], writes=[t.b for t in W])

    def cp(self, eng, out, in_, R=(), W=()):
        if eng == "act":
            return self.p.op("act", lambda e: e.copy(out=out, in_=in_), reads=[t.b for t in R], writes=[t.b for t in W])
        return self.p.op(eng, lambda e: e.tensor_copy(out=out, in_=in_), reads=[t.b for t in R], writes=[t.b for t in W])

    def memset(self, eng, out, val, W=()):
        return self.p.op(eng, lambda e: e.memset(out, val), writes=[t.b for t in W])

    def dma(self, q, out, in_, R=(), W=(), slow=False):
        if slow:
            f = lambda e: e.dma_start(out=out, in_=in_, allow_slow_non_contiguous=True)
        else:
            f = lambda e: e.dma_start(out=out, in_=in_)
        return self.p.op(q, f, reads=[t.b for t in R], writes=[t.b for t in W], dma=True)

    def aselect(self, out, in_, step, cm, cmp, fill, R=(), W=()):
        return self.p.op("pool", lambda e: e.affine_select(out=out, in_=in_, pattern=[[step, 128]], compare_op=cmp,
                                                           fill=fill, base=0, channel_multiplier=cm),
                         reads=[t.b for t in R], writes=[t.b for t in W])


def bc(ap, shape):
    return ap.to_broadcast(list(shape))


def build_program(TT, stop=None, dbg=False):
    nc = bass.Bass("TRN2", target_bir_lowering=False)
    HT = TT // 2
    NCH = TT // 128
    I = {}

    def inp(name, shape):
        I[name] = nc.dram_tensor(name, list(shape), F32, kind="ExternalInput").ap()

    inp("x", [TT, D]); inp("c2", [2, D]); inp("flag", [128, 1])
    inp("norm1_w", [1, D]); inp("w_in", [1, D, PIN]); inp("rwkv_mu", [1, 2, 3456]); inp("rwkv_w0", [1, 2, 1024])
    inp("rwkv_w2", [1, 2, 64, 1024]); inp("rwkv_a0", [1, 2, 1024]); inp("rwkv_a2", [1, 2, 64, 1024])
    inp("rwkv_g2", [1, 128, 1024]); inp("rwkv_k_k", [1, 1024]); inp("rwkv_k_a", [1, 1024]); inp("rwkv_r_k", [1, 16, 64])
    inp("rwkv_lnx_w", [1, 1024]); inp("rwkv_lnx_b", [1, 1024]); inp("ssm_conv_w", [1, 7, 1536]); inp("ssm_conv_b", [1, 1536])
    inp("ssm_dt_bias", [1, 2, 16]); inp("ssm_a_log", [1, 2, 16]); inp("ssm_d", [1, 16]); inp("ssm_norm_w", [1, 1024])
    inp("w_out", [1, D, D]); inp("norm2_w", [1, D]); inp("ffn_w_up", [1, D, 2 * DFF]); inp("ffn_conv_w", [1, 3, 2 * DFF])
    inp("ffn_conv_b", [1, 2 * DFF]); inp("ffn_w_down", [1, DFF, D]); inp("w_ada", [1, D, 6 * D]); inp("b_ada", [1, 6 * D])
    inp("final_norm_w", [D]); inp("w_ada_final", [D, 2 * D]); inp("b_ada_final", [2 * D])
    y_out = nc.dram_tensor("y", [TT, D], F32, kind="ExternalOutput").ap()

    with contextlib.ExitStack() as es:
        K = KB(nc, TT, es)
        K.dbg = dbg
        p = K.p

        def fin():
            p.barrier()
            p.run(engsems, dmasems, block)
            return nc

        def gstep(g, n=1):
            if g is None:
                return
            for _ in range(n):
                try:
                    next(g)
                except StopIteration:
                    return

        def drain(g):
            if g is not None:
                for _ in g:
                    pass

        OUTB = p.buf("yout", multi=True)
        PT = K.dram("PT", [PIN, TT], F32)
        YFR = K.dram("YFR", [TT, 1024], F32)
        YFS = K.dram("YFS", [TT, 1024], F32)
        MIXT = K.dram("MIXT", [D, TT], BF16)
        X1T = K.dram("X1T", [D, TT], F32)
        H2T = K.dram("H2T", [D, TT], BF16)
        identf = K.sb("identf", [128, 128], F32)
        identb = K.sb("identb", [128, 128], BF16)
        onesf = K.sb("onesf", [128, 128], F32)
        blk1 = K.sb("blk1", [128, 128], F32)
        UTs = K.sb("UTs", [128, 128], F32); UTi = K.sb("UTi", [128, 128], F32)
        LTs = K.sb("LTs", [128, 128], F32); LTi = K.sb("LTi", [128, 128], F32)
        flag = K.sb("flagt", [128, 1], F32)
        modT = K.sb("modT", [128, 2, 8, 16], F32)
        aff = K.sb("aff", [128, 2, 2, 6, 16], F32)
        cw = K.sb("cw", [128, 48, 8], F32)
        cbF = K.sb("cbF", [128, NCH, 16], F32)
        K.init_arena(50400)
        engsems = {e: es.enter_context(nc.semaphore("s_" + e)) for e in ENGS}
        dmasems = {(e, s): es.enter_context(nc.semaphore(f"d_{e}_{s}")) for e in ("sp", "act", "pool") for s in range(NSLOT)}
        block = es.enter_context(nc.Block())

        K.memset("pool", identf.ap, 0.0, W=[identf])
        K.aselect(identf.ap, identf.ap, -1, 1, ALU.not_equal, 1.0, R=[identf], W=[identf])
        K.cp("pool", identb.ap, identf.ap, R=[identf], W=[identb])
        K.memset("pool", onesf.ap, 1.0, W=[onesf])
        K.memset("pool", blk1.ap, 0.0, W=[blk1])
        K.memset("pool", blk1[0:64, 0:64], 1.0, W=[blk1])
        K.memset("pool", blk1[64:128, 64:128], 1.0, W=[blk1])
        for (m, step, cm, cmp) in ((UTs, 1, -1, ALU.is_gt), (UTi, 1, -1, ALU.is_ge), (LTs, -1, 1, ALU.is_gt), (LTi, -1, 1, ALU.is_ge)):
            K.aselect(m.ap, onesf.ap, step, cm, cmp, 0.0, R=[onesf], W=[m])
        K.dma("sp", flag.ap, I["flag"], W=[flag])

        def colvec(dst, src2d, n, dstT):
            st = K.carve("cvst", [128, 128], F32)
            K.dma("sp", st[0:n, :], src2d, W=[st])
            ps = K.ps()
            K.tr(ps[:, 0:n], st[0:n, :], identf[0:n, 0:n], R=[st, identf], W=[ps])
            K.cp("dve", dst, ps[:, 0:n], R=[ps], W=[dstT])

        WADAT = K.dram("WADAT", [32, 128, 16, 512], BF16)
        WINT = K.dram("WINT", [48, 128, 16, 128], BF16)
        WOUTT = K.dram("WOUTT", [128, 16, D], BF16)
        WUPT = K.dram("WUPT", [44, 128, 2, 16, 128], BF16)
        WDNT = K.dram("WDNT", [16, 128, 44, 128], BF16)
        for kc in range(16):
            rs = slice(kc * 128, (kc + 1) * 128)
            K.dma("pool", WADAT[0:24, :, kc, :].rearrange("g p c -> p g c"), I["w_ada"][0][rs, :].rearrange("p (g c) -> p g c", c=512), W=[WADAT])
            K.dma("pool", WADAT[24:32, :, kc, :].rearrange("g p c -> p g c"), I["w_ada_final"][rs, :].rearrange("p (g c) -> p g c", c=512), W=[WADAT])
        for kc in range(16):
            rs = slice(kc * 128, (kc + 1) * 128)
            K.dma("pool", WINT[0:47, :, kc, :].rearrange("g p c -> p g c"), I["w_in"][0][rs, 0:6016].rearrange("p (g c) -> p g c", c=128), W=[WINT])
            K.dma("pool", WINT[47, :, kc, 0:32], I["w_in"][0][rs, 6016:6048], W=[WINT])
        K.reset()
        c2t = K.carve("c2t", [2, D], F32)
        K.dma("sp", c2t.ap, I["c2"], W=[c2t])
        K.act(c2t.ap, c2t.ap, AF.Silu, R=[c2t], W=[c2t])
        scT = K.carve("scT", [128, 16, 2], BF16)
        ps = K.ps()
        for k in range(16):
            K.tr(ps[:, 2 * k:2 * k + 2], c2t[:, k * 128:(k + 1) * 128], identf[0:2, 0:2], R=[c2t, identf], W=[ps])
        K.cp("dve", scT.ap, ps[:, 0:32].rearrange("p (k h) -> p k h", h=2), R=[ps], W=[scT])
        badaT = K.carve("badaT", [128, 128], F32)
        colvec(badaT[:, 0:96], I["b_ada"][0].rearrange("(c p) -> c p", p=128), 96, badaT)
        colvec(badaT[:, 96:128], I["b_ada_final"].rearrange("(c p) -> c p", p=128), 32, badaT)
        wab = [K.carve(f"wab{i}", [128, 16, 512], BF16) for i in range(3)]

        nblk = 24 + 8
        def ld_ada(g):
            K.dma("act", wab[g % 3].ap, WADAT[g], R=[WADAT], W=[wab[g % 3]])
        ld_ada(0); ld_ada(1)
        for g in range(nblk):
            wt = wab[g % 3]
            if g + 2 < nblk:
                ld_ada(g + 2)
            if g % 4 == 0:
                ps = K.ps()
            for jl in range(4):
                j = (g % 4) * 4 + jl
                for k in range(16):
                    K.mm(ps[:, 2 * j:2 * j + 2], wt[:, k, jl * 128:(jl + 1) * 128], scT[:, k, :], start=(k == 0), stop=(k == 15),
                         R=[wt, scT], W=[ps])
            if g % 4 == 3:
                v = g // 4
                K.tt("dve", modT[:, :, v, :], ps[:, 0:32].rearrange("p (j h) -> p h j", h=2),
                     bc(badaT[:, v * 16:(v + 1) * 16].unsqueeze(1), [128, 2, 16]), ALU.add, R=[ps, badaT], W=[modT])
        nwT = K.carve("nwT", [128, 3, 16], F32)
        colvec(nwT[:, 0, :], I["norm1_w"][0].rearrange("(c p) -> c p", p=128), 16, nwT)
        colvec(nwT[:, 1, :], I["norm2_w"][0].rearrange("(c p) -> c p", p=128), 16, nwT)
        colvec(nwT[:, 2, :], I["final_norm_w"].rearrange("(c p) -> c p", p=128), 16, nwT)
        for hlf in range(2):
            for i, (vsc, vsh) in enumerate(((1, 0), (4, 3), (7, 6))):
                K.ts("dve", aff[:, 0, hlf, 2 * i, :], modT[:, hlf, vsc, :], 1.0, None, ALU.add, R=[modT], W=[aff])
                K.tt("dve", aff[:, 0, hlf, 2 * i, :], aff[:, 0, hlf, 2 * i, :], nwT[:, i, :], ALU.mult, R=[aff, nwT], W=[aff])
                K.cp("dve", aff[:, 0, hlf, 2 * i + 1, :], modT[:, hlf, vsh, :], R=[modT], W=[aff])
        K.ts("dve", aff[:, 1], aff[:, 0], flag[:, 0:1], None, ALU.mult, R=[aff, flag], W=[aff])

        K.memset("pool", cw.ap, 0.0, W=[cw])
        ps = K.ps()
        st = K.carve("cwst", [128, 128], F32)
        K.dma("sp", st[0:54, :], I["rwkv_mu"][0].rearrange("j (c p) -> (j c) p", p=128), W=[st])
        K.tr(ps[:, 0:54], st[0:54, :], identf[0:54, 0:54], R=[st, identf], W=[ps])
        K.cp("dve", cw[:, 0:27, 0], ps[:, 0:27], R=[ps], W=[cw])
        K.cp("dve", cw[:, 0:27, 2], ps[:, 27:54], R=[ps], W=[cw])
        K.tt("dve", cw[:, 0:27, 1], cw[:, 0:27, 0], cw[:, 0:27, 2], ALU.add, R=[cw], W=[cw])
        K.ts("dve", cw[:, 0:27, 1], cw[:, 0:27, 1], -1.0, 1.0, ALU.mult, ALU.add, R=[cw], W=[cw])
        ps = K.ps()
        st2 = K.carve("cwst2", [128, 128], F32)
        K.dma("sp", st2[0:84, :], I["ssm_conv_w"][0].rearrange("j (c p) -> (j c) p", p=128), W=[st2])
        K.tr(ps[:, 0:84], st2[0:84, :], identf[0:84, 0:84], R=[st2, identf], W=[ps])
        K.cp("dve", cw[:, 35:47, 0:7], ps[:, 0:84].rearrange("p (j c) -> p c j", c=12), R=[ps], W=[cw])
        colvec(cw[:, 35:47, 7], I["ssm_conv_b"][0].rearrange("(c p) -> c p", p=128), 12, cw)
        K.dma("sp", cw[0:32, 47, 7:8], I["ssm_dt_bias"].rearrange("a d h -> (d h) a"), W=[cw])

        if stop == "0":
            return fin()
        K.reset()
        ST = min(1024, HT)
        NSUB = ST // 256
        hTs = [K.carve(f"hT{i}", [128, 16, ST + 6], BF16) for i in range(2)]
        xts = [K.carve(f"xt{i}", [128, D], F32) for i in range(2)]
        junk = K.carve("junk", [128, D], BF16)
        st1 = [K.carve(f"st1_{i}", [128, 2], F32) for i in range(2)]
        winb = [K.carve(f"winb{i}", [128, 16, 128], BF16) for i in range(6)]
        s3 = list(range(27)); zz = list(range(27, 35)) + [47]; c7 = list(range(35, 47))
        sched = []
        for i_ in range(12):
            sched.append(c7[i_])
            sched += [s3.pop(0), s3.pop(0)]
            sched.append(zz.pop(0) if (i_ % 4 != 3 and zz) else (s3.pop(0) if s3 else zz.pop(0)))
        sched += s3 + zz
        assert sorted(sched) == list(range(48)), sched
        raws = [K.carve(f"raw{i}", [128, 262], F32) for i in range(3)]
        accs = [K.carve(f"acc{i}", [128, 256], F32) for i in range(3)]
        outs = [K.carve(f"outb{i}", [128, 256], F32) for i in range(3)]

        cnt = {"x": 0, "e": 0}

        def make_hT(dst, tok0, n, dcol, hlf, flagged, si, bi):
            i = cnt["x"] % 2
            cnt["x"] += 1
            xt, s1 = xts[i], st1[i]
            K.dma("sp", xt[0:n, :], I["x"][tok0:tok0 + n, :], W=[xt])
            K.act(junk[0:n, :], xt[0:n, :], AF.Square, accum=s1[0:n, 0:1], R=[xt], W=[junk, s1])
            K.rsqrt(s1[0:n, 1:2], s1[0:n, 0:1], 1.0 / D, 1e-6, R=[s1], W=[s1])
            K.act(xt[0:n, :], xt[0:n, :], AF.Identity, scale=s1[0:n, 1:2], R=[xt, s1], W=[xt])
            fl = 1 if flagged else 0
            for q in range(4):
                ps = K.ps()
                for kk in range(4):
                    k = q * 4 + kk
                    K.tr(ps[:, kk * 128:kk * 128 + n], xt[0:n, k * 128:(k + 1) * 128], identf[0:n, 0:n], R=[xt, identf], W=[ps])
                for kk in range(4):
                    k = q * 4 + kk
                    K.act(dst[:, k, dcol:dcol + n], ps[:, kk * 128:kk * 128 + n], AF.Identity,
                          bias=aff[:, fl, hlf, bi, k:k + 1], scale=aff[:, fl, hlf, si, k:k + 1], R=[ps, aff], W=[dst])

        def fill_halo(dst, t0, width, nh, si, bi):
            hlf = t0 // HT
            if t0 == 0:
                K.memset("pool", dst[:, :, 0:nh], 0.0, W=[dst])
            elif t0 % HT == 0:
                make_hT(dst, t0 - nh, nh, 0, hlf - 1, True, si, bi)
            else:
                make_hT(dst, t0 - nh, nh, 0, hlf, False, si, bi)
            tR = t0 + width
            if tR == TT:
                K.memset("pool", dst[:, :, nh + width:nh + width + nh], 0.0, W=[dst])
            elif tR % HT == 0:
                make_hT(dst, tR, nh, nh + width, hlf + 1, True, si, bi)
            else:
                make_hT(dst, tR, nh, nh + width, hlf, False, si, bi)

        def ld_win(n):
            n = n % 48
            K.dma("act", winb[wcnt["n"] % 6].ap, WINT[sched[n]], R=[WINT], W=[winb[wcnt["n"] % 6]])
            wcnt["n"] += 1
        wcnt = {"n": 0}

        def gen_hT(sti_):
            dst_ = hTs[sti_ % 2]
            t0_ = sti_ * ST
            for b in range(ST // 128):
                make_hT(dst_, t0_ + b * 128, 128, 3 + b * 128, t0_ // HT, False, 0, 1)
                yield
            fill_halo(dst_, t0_, ST, 3, 0, 1)
            yield

        drain(gen_hT(0))
        for sti in range(TT // ST):
            t0 = sti * ST
            hlf = t0 // HT
            hT = hTs[sti % 2]
            nxt_h = gen_hT(sti + 1) if sti + 1 < TT // ST else None
            if sti == 0:
                for n_ in range(4):
                    ld_win(n_)
            for si_, cc in enumerate(sched):
                wt = winb[(sti * 48 + si_) % 6]
                if si_ + 4 < 48 or sti + 1 < TT // ST:
                    ld_win(si_ + 4)
                if si_ % 4 == 0:
                    gstep(nxt_h)
                if True:
                    ccl = 0
                    M = 128 if cc < 47 else 32
                    for j in range(NSUB):
                        ps = K.ps()
                        for k in range(16):
                            K.mm(ps[0:M, 0:262], wt[:, k, ccl * 128:ccl * 128 + M], hT[:, k, j * 256:j * 256 + 262],
                                 start=(k == 0), stop=(k == 15), R=[wt, hT], W=[ps])
                        i = cnt["e"] % 3
                        cnt["e"] += 1
                        raw, acc, ob = raws[i], accs[i], outs[i]
                        if cc < 27:
                            K.cp("act", raw.ap, ps[:, 0:262], R=[ps], W=[raw])
                            K.ts("dve", ob.ap, raw[:, 2:258], cw[:, cc, 0:1], None, ALU.mult, R=[raw, cw], W=[ob])
                            K.stt(ob.ap, raw[:, 3:259], cw[:, cc, 1:2], ob.ap, ALU.mult, ALU.add, R=[raw, cw, ob], W=[ob])
                            K.stt(ob.ap, raw[:, 4:260], cw[:, cc, 2:3], ob.ap, ALU.mult, ALU.add, R=[raw, cw, ob], W=[ob])
                        elif cc < 35:
                            K.act(ob.ap, ps[:, 3:259], AF.Silu, R=[ps], W=[ob])
                        elif cc < 47:
                            K.cp("act", raw.ap, ps[:, 0:262], R=[ps], W=[raw])
                            K.ts("dve", acc.ap, raw[:, 0:256], cw[:, cc, 0:1], cw[:, cc, 7:8], ALU.mult, ALU.add, R=[raw, cw], W=[acc])
                            for tp in range(1, 7):
                                K.stt(acc.ap, raw[:, tp:tp + 256], cw[:, cc, tp:tp + 1], acc.ap, ALU.mult, ALU.add, R=[raw, cw, acc], W=[acc])
                            K.act(ob.ap, acc.ap, AF.Silu, R=[acc], W=[ob])
                        else:
                            K.act(acc[0:32, :], ps[0:32, 3:259], AF.Exp, bias=cw[0:32, 47, 7:8], R=[ps, cw], W=[acc])
                            K.act(ob[0:32, :], acc[0:32, :], AF.Ln, bias=1.0, R=[acc], W=[ob])
                        K.dma("sp", PT[cc * 128:cc * 128 + M, t0 + j * 256:t0 + (j + 1) * 256], ob[0:M, :], R=[ob], W=[PT])
            drain(nxt_h)

        if stop == "A":
            return fin()
        K.reset()
        for kc in range(16):
            rs = slice(kc * 128, (kc + 1) * 128)
            K.dma("pool", WOUTT[:, kc, :], I["w_out"][0][rs, :], W=[WOUTT])
        for kc in range(16):
            rs = slice(kc * 128, (kc + 1) * 128)
            for part in range(2):
                K.dma("pool", WUPT[:, :, part, kc, :].rearrange("j p c -> p j c"),
                      I["ffn_w_up"][0][rs, part * DFF:(part + 1) * DFF].rearrange("p (j c) -> p j c", c=128), W=[WUPT])
        for j in range(44):
            K.dma("pool", WDNT[:, :, j, :].rearrange("m p c -> p m c"), I["ffn_w_down"][0][j * 128:(j + 1) * 128, :].rearrange("p (m c) -> p m c", c=128), W=[WDNT])
        kkc = K.carve("kkc", [128, 8], F32); kac = K.carve("kac", [128, 8], F32); kam1 = K.carve("kam1", [128, 8], F32)
        w0c = K.carve("w0c", [128, 16], F32); a0c = K.carve("a0c", [128, 16], F32)
        colvec(kkc.ap, I["rwkv_k_k"][0].rearrange("(c p) -> c p", p=128), 8, kkc)
        colvec(kac.ap, I["rwkv_k_a"][0].rearrange("(c p) -> c p", p=128), 8, kac)
        K.ts("dve", kam1.ap, kac.ap, -1.0, None, ALU.add, R=[kac], W=[kam1])
        colvec(w0c.ap, I["rwkv_w0"][0].rearrange("d (c p) -> (d c) p", p=128), 16, w0c)
        colvec(a0c.ap, I["rwkv_a0"][0].rearrange("d (c p) -> (d c) p", p=128), 16, a0c)
        nbias = K.carve("nbias", [128, 32], F32)
        K.ts("dve", nbias[:, 0:16], w0c.ap, -1.0, None, ALU.mult, R=[w0c], W=[nbias])
        K.ts("dve", nbias[:, 16:32], a0c.ap, -1.0, None, ALU.mult, R=[a0c], W=[nbias])
        w2b = K.carve("w2b", [64, 2, 1024], BF16); a2b = K.carve("a2b", [64, 2, 1024], BF16); g2b = K.carve("g2b", [128, 1024], BF16)
        K.dma("pool", w2b.ap, I["rwkv_w2"][0].rearrange("d r c -> r d c"), W=[w2b])
        K.dma("pool", a2b.ap, I["rwkv_a2"][0].rearrange("d r c -> r d c"), W=[a2b])
        K.dma("pool", g2b.ap, I["rwkv_g2"][0], W=[g2b])
        rkblk = K.carve("rkblk", [128, 8, 2], F32)
        K.memset("pool", rkblk.ap, 0.0, W=[rkblk])
        rk_r = I["rwkv_r_k"][0].rearrange("(cc j) n -> j n cc", j=2)
        for j in range(2):
            K.dma("sp", rkblk[64 * j:64 * j + 64, :, j], rk_r[j], W=[rkblk], slow=True)
        lnw = K.carve("lnw", [128, 1024], F32); lnb = K.carve("lnb", [128, 1024], F32)
        K.dma("sp", lnw.ap, I["rwkv_lnx_w"][0].partition_broadcast(128), W=[lnw])
        K.dma("sp", lnb.ap, I["rwkv_lnx_b"][0].partition_broadcast(128), W=[lnb])
        rmask = K.carve("rmask", [128, 8, 128], F32)
        K.memset("pool", rmask.ap, 1.0, W=[rmask])
        K.memset("pool", rmask[:, :, 0:1], 0.0, W=[rmask])
        M4 = [K.carve(f"M4_{d}", [128, 4, 128], F32) for d in range(2)]
        for d, (ms, mi) in enumerate(((UTs, UTi), (LTs, LTi))):
            for q, m in enumerate((ms, mi, ms, mi)):
                K.cp("pool", M4[d][:, q, :], m.ap, R=[m], W=[M4[d]])
        MQ = [LTs, UTs]
        S32 = K.carve("S32", [128, 8, 64], F32); Sbf = K.carve("Sbf", [128, 8, 64], BF16)
        rkv = K.carve("rkv", [128, 3, 8, 128], F32)
        wdad = K.carve("wdad", [64, 2, 128], F32); wdadb = K.carve("wdadb", [64, 2, 128], BF16)
        gdt = K.carve("gdt", [128, 128], F32); gdb = K.carve("gdb", [128, 128], BF16)
        off_sg = K.aoff
        sg = K.carve("sg", [128, 8, 128], F32); ic = K.carve("ic", [128, 8, 128], F32)
        cum = K.carve("cum", [128, 8, 128], F32); cumx = K.carve("cumx", [128, 8, 128], F32)
        Q1v = T(K.arena[:, off_sg:off_sg + 2048].rearrange("p (a b) -> p a b", a=16), sg.b)
        QT1v = T(K.arena[:, off_sg + 2048:off_sg + 4096].rearrange("p (a b) -> p a b", a=16), cum.b)
        Wt = K.carve("Wt", [128, 8, 128], F32); Winv = K.carve("Winv", [128, 8, 128], F32); Wx = K.carve("Wx", [128, 8, 128], F32)
        kr = K.carve("kr", [128, 8, 128], F32); tmpA = K.carve("tmpA", [128, 8, 128], F32); tmpB = K.carve("tmpB", [128, 8, 128], F32)
        kkn = K.carve("kkn", [128, 8, 128], F32); kd = K.carve("kd", [128, 8, 128], F32)
        ar = K.carve("ar", [128, 2, 8, 128], BF16); bk = K.carve("bk", [128, 2, 8, 128], BF16); btk = K.carve("btk", [128, 2, 8, 128], BF16)
        BKtok = K.carve("BKtok", [128, 2, 1024], BF16); Vtok = K.carve("Vtok", [128, 1024], BF16); Vtok32 = K.carve("Vtok32", [128, 1024], F32)
        AT = K.carve("AT", [128, 16, 512], BF16)
        Qa = [K.carve("Q0f", [128, 16, 128], F32), Q1v]
        QTa = [K.carve("QT0f", [128, 16, 128], F32), QT1v]
        P32g = [K.carve(f"P32_{i}", [128, 4, 128], F32) for i in range(4)]
        Pbf = K.carve("Pbf", [128, 16, 128], BF16)
        X1 = K.carve("X1", [128, 1024], BF16); SAb = K.carve("SAb", [128, 1024], BF16)
        ysb = K.carve("ysb", [128, 1024], F32); yfl = K.carve("yfl", [128, 1024], F32)
        st16 = K.carve("st16", [128, 6, 16], F32)
        tot8 = K.carve("tot8", [128, 8, 1], F32)
        cm8 = K.carve("cm8", [128, 8, 1], F32); cL8 = K.carve("cL8", [128, 8, 1], F32)
        T1 = K.carve("T1", [128, 8, 128], F32)
        arc = K.carve("arc", [128, 2, 8, 128], BF16)
        ob16 = K.carve("ob16", [128, 1024], BF16); mixo = K.carve("mixo", [128, 8, 128], BF16)
        PT_r = PT.ap
        WL8 = K.carve("WL8", [128, 8, 1], F32)
        st5 = [K.carve(f"st5_{i}", [128, 16], F32) for i in range(2)]
        Qf, QTf = Qa[0], QTa[0]
        Qg = [T(Qf[:, hq * 4:(hq + 1) * 4, :], p.buf(f"Qg{hq}")) for hq in range(4)]
        QTg = [T(QTf[:, hq * 4:(hq + 1) * 4, :], p.buf(f"QTg{hq}")) for hq in range(4)]

        def prepP(d, c, par):
            tk = c * 128
            for q, row in enumerate((ROW_R, ROW_K, ROW_V)):
                K.dma("sp", rkv[:, q], PT_r[row:row + 1024, tk:tk + 128].rearrange("(cc p) t -> p cc t", p=128), R=[PT], W=[rkv])
            K.dma("sp", wdad[:, 0, :], PT_r[ROW_WD + d * 64:ROW_WD + d * 64 + 64, tk:tk + 128], R=[PT], W=[wdad])
            K.dma("sp", wdad[:, 1, :], PT_r[ROW_AD + d * 64:ROW_AD + d * 64 + 64, tk:tk + 128], R=[PT], W=[wdad])
            r_, k_, v_ = rkv[:, 0], rkv[:, 1], rkv[:, 2]
            yield
            K.act(wdadb[:, 0, :], wdad[:, 0, :], AF.Tanh, R=[wdad], W=[wdadb])
            K.cp("act", wdadb[:, 1, :], wdad[:, 1, :], R=[wdad], W=[wdadb])
            K.tt("pool", kr.ap, k_, bc(kkc.ap.unsqueeze(2), [128, 8, 128]), ALU.mult, R=[rkv, kkc], W=[kr])
            yield
            K.tt("pool", tmpA.ap, kr.ap, kr.ap, ALU.mult, R=[kr], W=[tmpA])
            yield
            for (wsrc, xi, dstt) in ((w2b, 0, sg), (a2b, 1, ic)):
                pss = [K.ps(), K.ps()]
                for cc in range(8):
                    K.mm(pss[cc // 4][:, (cc % 4) * 128:(cc % 4 + 1) * 128], wsrc[:, d, cc * 128:(cc + 1) * 128], wdadb[:, xi, :],
                         R=[wsrc, wdadb], W=[pss[cc // 4]])
                yield
                bsrc = w0c if xi == 0 else a0c
                for cc in range(8):
                    K.act(dstt[:, cc, :], pss[cc // 4][:, (cc % 4) * 128:(cc % 4 + 1) * 128], AF.Sigmoid,
                          bias=bsrc[:, d * 8 + cc:d * 8 + cc + 1], R=[pss[cc // 4], bsrc], W=[dstt])
                    if cc % 4 == 3:
                        yield
            pss = [K.ps(), K.ps()]
            for cc in range(8):
                K.mm(pss[cc // 4][:, (cc % 4) * 128:(cc % 4 + 1) * 128], blk1.ap, tmpA[:, cc, :], R=[blk1, tmpA], W=[pss[cc // 4]])
            yield
            for hb in range(2):
                K.rsqrt(tmpB[:, hb * 4:(hb + 1) * 4, :], pss[hb].ap.rearrange("p (a b) -> p a b", a=4), 1.0, 1e-24, R=[pss[hb]], W=[tmpB])
                yield
            K.tt("dve", kkn.ap, kr.ap, tmpB.ap, ALU.mult, R=[kr, tmpB], W=[kkn])
            yield
            sgf = sg.ap.rearrange("p a b -> p (a b)")
            cumf = cum.ap.rearrange("p a b -> p (a b)")
            p.op("dve", (lambda o_, m_, s_: lambda e: e.tensor_tensor_scan(out=o_, data0=m_, data1=s_, initial=0.0, op0=ALU.mult, op1=ALU.add))(
                cumf, rmask.ap.rearrange("p a b -> p (a b)"), sgf), reads=[rmask.b, sg.b], writes=[cum.b])
            yield
            if d == 1:
                K.tt("dve", tmpA.ap, sg.ap, cum.ap, ALU.subtract, R=[sg, cum], W=[tmpA])
                K.cp("dve", tot8.ap, cum[:, :, 127:128], R=[cum], W=[tot8])
                yield
                K.tt("dve", cum.ap, tmpA.ap, bc(tot8.ap, [128, 8, 128]), ALU.add, R=[tmpA, tot8], W=[cum])
                yield
            K.tt("dve", cumx.ap, cum.ap, sg.ap, ALU.subtract, R=[cum, sg], W=[cumx])
            eidx = 127 if d == 0 else 0
            K.act(Wt.ap, cum.ap, AF.Exp, scale=-DEC_C, R=[cum], W=[Wt])
            yield
            K.act(Wx.ap, cumx.ap, AF.Exp, scale=-DEC_C, R=[cumx], W=[Wx])
            K.cp("dve", cm8.ap, cum[:, :, 64:65], R=[cum], W=[cm8])
            K.cp("dve", cL8.ap, cum[:, :, eidx:eidx + 1], R=[cum], W=[cL8])
            yield
            K.tt("dve", T1.ap, cum.ap, bc(cm8.ap, [128, 8, 128]), ALU.subtract, R=[cum, cm8], W=[T1])
            yield
            K.tt("dve", cumx.ap, cumx.ap, bc(cm8.ap, [128, 8, 128]), ALU.subtract, R=[cumx, cm8], W=[cumx])
            K.act(Winv.ap, T1.ap, AF.Exp, scale=DEC_C, R=[T1], W=[Winv])
            yield
            K.act(T1.ap, T1.ap, AF.Exp, scale=-DEC_C, R=[T1], W=[T1])
            K.act(cumx.ap, cumx.ap, AF.Exp, scale=-DEC_C, R=[cumx], W=[cumx])
            K.tt("dve", cum.ap, cum.ap, bc(cL8.ap, [128, 8, 128]), ALU.subtract, R=[cum, cL8], W=[cum])
            yield
            K.act(cum.ap, cum.ap, AF.Exp, scale=DEC_C, R=[cum], W=[cum])
            K.tt("pool", tmpA.ap, ic.ap, bc(kac.ap.unsqueeze(2), [128, 8, 128]), ALU.mult, R=[ic, kac], W=[tmpA])
            yield
            K.tt("pool", tmpA.ap, tmpA.ap, bc(kam1.ap.unsqueeze(2), [128, 8, 128]), ALU.subtract, R=[tmpA, kam1], W=[tmpA])
            yield
            K.tt("dve", kd.ap, k_, tmpA.ap, ALU.mult, R=[rkv, tmpA], W=[kd])
            yield
            K.tt("dve", tmpB.ap, kkn.ap, ic.ap, ALU.mult, R=[kkn, ic], W=[tmpB])
            K.tt("pool", tmpA.ap, r_, kd.ap, ALU.mult, R=[rkv, kd], W=[tmpA])
            yield
            psb = K.ps()
            for cc in range(8):
                K.mm(psb[:, 2 * cc:2 * cc + 2], tmpA[:, cc, :], rkblk[:, cc, :], R=[tmpA, rkblk], W=[psb])
            if d == 0:
                K.cp("act", cbF[:, c, :], psb[:, 0:16], R=[psb], W=[cbF])
            else:
                K.tt("dve", st5[par].ap, psb[:, 0:16], cbF[:, c, :], ALU.add, R=[psb, cbF], W=[st5[par]])
            yield

        for d in range(2):
            K.memset("pool", S32.ap, 0.0, W=[S32])
            K.memset("pool", Sbf.ap, 0.0, W=[Sbf])
            order = list(range(NCH)) if d == 0 else list(range(NCH - 1, -1, -1))
            drain(prepP(d, order[0], 0))
            for ci, c in enumerate(order):
                tk = c * 128
                eidx = 127 if d == 0 else 0
                r_, k_, v_ = rkv[:, 0], rkv[:, 1], rkv[:, 2]
                if ci == NCH // 2:
                    K.ts("dve", S32.ap, S32.ap, flag[:, 0:1], None, ALU.mult, R=[S32, flag], W=[S32])
                    K.cp("act", Sbf.ap, S32.ap, R=[S32], W=[Sbf])
                K.cp("dve", WL8.ap, Wt[:, :, eidx:eidx + 1], R=[Wt], W=[WL8])
                K.stt(ar[:, 0], kkn.ap, -1.0, Wx.ap, ALU.mult, ALU.mult, R=[kkn, Wx], W=[ar])
                K.tt("pool", ar[:, 1], r_, Wt.ap, ALU.mult, R=[rkv, Wt], W=[ar])
                K.stt(arc[:, 0], kkn.ap, -1.0, cumx.ap, ALU.mult, ALU.mult, R=[kkn, cumx], W=[arc])
                K.tt("pool", arc[:, 1], r_, T1.ap, ALU.mult, R=[rkv, T1], W=[arc])
                K.tt("dve", bk[:, 0], tmpB.ap, Winv.ap, ALU.mult, R=[tmpB, Winv], W=[bk])
                K.tt("pool", bk[:, 1], kd.ap, Winv.ap, ALU.mult, R=[kd, Winv], W=[bk])
                K.tt("dve", btk[:, 0], tmpB.ap, cum.ap, ALU.mult, R=[tmpB, cum], W=[btk])
                K.tt("pool", btk[:, 1], kd.ap, cum.ap, ALU.mult, R=[kd, cum], W=[btk])
                for x_ in range(2):
                    psT = K.ps()
                    psTb = psT.ap.bitcast(BF16)
                    for cc in range(8):
                        K.tr(psTb[:, cc * 128:(cc + 1) * 128], btk[:, x_, cc, :], identb.ap, R=[btk, identb], W=[psT])
                    K.cp("act" if x_ == 0 else "dve", BKtok[:, x_, :], psTb, R=[psT], W=[BKtok])
                pss = [K.ps(), K.ps()]
                for cc in range(8):
                    K.tr(pss[cc // 4][:, (cc % 4) * 128:(cc % 4 + 1) * 128], rkv[:, 2, cc, :], identf.ap, R=[rkv, identf], W=[pss[cc // 4]])
                for hb in range(2):
                    K.cp("act", Vtok[:, hb * 512:(hb + 1) * 512], pss[hb].ap, R=[pss[hb]], W=[Vtok])
                    if d == 1:
                        K.cp("act", Vtok32[:, hb * 512:(hb + 1) * 512], pss[hb].ap, R=[pss[hb]], W=[Vtok32])
                for h in range(16):
                    cc, pb = h // 2, 64 * (h % 2)
                    psH = K.ps()
                    rhs = arc[pb:pb + 64, :, cc, :]
                    K.mm(psH[:, 0:256], bk[pb:pb + 64, 0, cc, :], rhs, R=[bk, arc], W=[psH])
                    K.mm(psH[:, 256:512], bk[pb:pb + 64, 1, cc, :], rhs, R=[bk, arc], W=[psH])
                    K.tt("dve", AT[:, h, :], psH.ap, M4[d].ap.rearrange("p a b -> p (a b)"), ALU.mult, R=[psH, M4[d]], W=[AT])
                    K.tt("dve", QTf[:, h, :], psH[:, 0:128], M4[d][:, 0, :], ALU.mult, R=[psH, M4[d]], W=[QTg[h // 4]])
                Q4 = Qf.ap.rearrange("p (c j) t -> p c j t", j=2)
                for hq in range(4):
                    psQ = K.ps()
                    par, c0 = hq % 2, 4 * (hq // 2)
                    pb = 64 * par
                    for hh in range(4):
                        cc = c0 + hh
                        K.mm(psQ[:, hh * 128:(hh + 1) * 128], arc[pb:pb + 64, 0, cc, :], bk[pb:pb + 64, 0, cc, :], R=[arc, bk], W=[psQ])
                    K.tt("dve", Q4[:, c0:c0 + 4, par, :], psQ.ap.rearrange("p (a b) -> p a b", a=4),
                         bc(MQ[d].ap.unsqueeze(1), [128, 4, 128]), ALU.mult, R=[psQ, MQ[d]], W=[Qg[c0 // 2], Qg[c0 // 2 + 1]])
                for hq in range(4):
                    K.tt("dve", P32g[hq].ap, QTg[hq].ap, bc(identf.ap.unsqueeze(1), [128, 4, 128]), ALU.add, R=[QTg[hq], identf], W=[P32g[hq]])
                nxt = prepP(d, order[ci + 1], (ci + 1) % 2) if ci + 1 < NCH else None
                for lvl in range(1, 7):
                    for hq in range(4):
                        psa = K.ps()
                        for hh in range(4):
                            K.mm(psa[:, hh * 128:(hh + 1) * 128], QTg[hq][:, hh, :], Qg[hq][:, hh, :], R=[QTg[hq], Qg[hq]], W=[psa])
                        if lvl < 6:
                            psb_ = K.ps()
                            for hh in range(4):
                                K.mm(psb_[:, hh * 128:(hh + 1) * 128], Qg[hq][:, hh, :], QTg[hq][:, hh, :], R=[QTg[hq], Qg[hq]], W=[psb_])
                        K.cp("act", Qg[hq].ap, psa.ap.rearrange("p (a b) -> p a b", a=4), R=[psa], W=[Qg[hq]])
                        if lvl < 6:
                            K.cp("act", QTg[hq].ap, psb_.ap.rearrange("p (a b) -> p a b", a=4), R=[psb_], W=[QTg[hq]])
                        gstep(nxt)
                    for hq in range(4):
                        psq = K.ps()
                        for hh in range(4):
                            K.mm(psq[:, hh * 128:(hh + 1) * 128], Qg[hq][:, hh, :], P32g[hq][:, hh, :], R=[Qg[hq], P32g[hq]], W=[psq])
                        K.tt("dve", P32g[hq].ap, P32g[hq].ap, psq.ap.rearrange("p (a b) -> p a b", a=4),
                             ALU.add, R=[psq, P32g[hq]], W=[P32g[hq]])
                        gstep(nxt)
                for hq in range(4):
                    K.cp("pool", Pbf[:, hq * 4:(hq + 1) * 4, :], P32g[hq].ap, R=[P32g[hq]], W=[Pbf])
                drain(nxt)
                pss = [K.ps(), K.ps()]
                for h in range(16):
                    cc, pb = h // 2, 64 * (h % 2)
                    o_ = pss[h % 2][:, cc * 64:(cc + 1) * 64]
                    K.mm(o_, AT[:, h, 256:384], Vtok[:, h * 64:(h + 1) * 64], start=True, stop=False, R=[AT, Vtok], W=[pss[h % 2]])
                    K.mm(o_, ar[pb:pb + 64, 0, cc, :], Sbf[pb:pb + 64, cc, :], start=False, stop=True, R=[ar, Sbf], W=[pss[h % 2]])
                for hb in range(2):
                    K.cp("act", X1.ap.rearrange("p (c j n) -> p c j n", j=2, n=64)[:, :, hb, :], pss[hb].ap.rearrange("p (c n) -> p c n", n=64), R=[pss[hb]], W=[X1])
                pss = [K.ps(), K.ps()]
                for h in range(16):
                    K.mm(pss[h // 8][:, (h % 8) * 64:(h % 8 + 1) * 64], Pbf[:, h, :], X1[:, h * 64:(h + 1) * 64], R=[Pbf, X1], W=[pss[h // 8]])
                for hb in range(2):
                    K.cp("dve" if hb else "act", SAb[:, hb * 512:(hb + 1) * 512], pss[hb].ap, R=[pss[hb]], W=[SAb])
                pss = [K.ps(), K.ps()]
                for h in range(16):
                    cc, pb = h // 2, 64 * (h % 2)
                    o_ = pss[h % 2][:, cc * 64:(cc + 1) * 64]
                    K.mm(o_, ar[pb:pb + 64, 1, cc, :], Sbf[pb:pb + 64, cc, :], start=True, stop=False, R=[ar, Sbf], W=[pss[h % 2]])
                    K.mm(o_, AT[:, h, 128:256], SAb[:, h * 64:(h + 1) * 64], start=False, stop=False, R=[AT, SAb], W=[pss[h % 2]])
                    K.mm(o_, AT[:, h, 384:512], Vtok[:, h * 64:(h + 1) * 64], start=False, stop=True, R=[AT, Vtok], W=[pss[h % 2]])
                for hb in range(2):
                    K.cp("act" if hb else "dve", ysb.ap.rearrange("p (c j n) -> p c j n", j=2, n=64)[:, :, hb, :], pss[hb].ap.rearrange("p (c n) -> p c n", n=64), R=[pss[hb]], W=[ysb])
                pss = [K.ps(), K.ps()]
                for cc in range(8):
                    o_ = pss[cc // 4][:, (cc % 4) * 128:(cc % 4 + 1) * 128]
                    K.mm(o_, BKtok[:, 0, cc * 128:(cc + 1) * 128], SAb[:, cc * 128:(cc + 1) * 128], start=True, stop=False, R=[BKtok, SAb], W=[pss[cc // 4]])
                    K.mm(o_, BKtok[:, 1, cc * 128:(cc + 1) * 128], Vtok[:, cc * 128:(cc + 1) * 128], start=False, stop=True, R=[BKtok, Vtok], W=[pss[cc // 4]])
                K.tt("dve", S32.ap, S32.ap, bc(WL8.ap, [128, 8, 64]), ALU.mult, R=[S32, WL8], W=[S32])
                for hb in range(2):
                    v4 = pss[hb].ap.rearrange("p (a b) -> p a b", a=4)
                    K.tt("dve", S32[0:64, hb * 4:(hb + 1) * 4, :], S32[0:64, hb * 4:(hb + 1) * 4, :], v4[0:64, :, 0:64], ALU.add, R=[pss[hb], S32], W=[S32])
                    K.tt("dve", S32[64:128, hb * 4:(hb + 1) * 4, :], S32[64:128, hb * 4:(hb + 1) * 4, :], v4[64:128, :, 64:128], ALU.add, R=[pss[hb], S32], W=[S32])
                K.cp("act", Sbf.ap, S32.ap, R=[S32], W=[Sbf])
                if stop == "B5" or (d == 1 and stop == "B5b"):
                    return fin()
                if d == 0:
                    K.dma("sp", YFR[tk:tk + 128, :], ysb.ap, R=[ysb], W=[YFR])
                    continue
                if stop == "B7":
                    return fin()
                K.dma("sp", yfl.ap, YFR[tk:tk + 128, :], R=[YFR], W=[yfl])
                K.dma("sp", gdt.ap, PT_r[ROW_GD:ROW_GD + 128, tk:tk + 128], R=[PT], W=[gdt])
                K.act(gdb.ap, gdt.ap, AF.Sigmoid, R=[gdt], W=[gdb])
                K.tt("dve", ysb.ap, ysb.ap, yfl.ap, ALU.add, R=[ysb, yfl], W=[ysb])
                y3 = ysb.ap.rearrange("p (h n) -> p h n", h=16)
                p.op("dve", (lambda o_, i_: lambda e: e.tensor_reduce(out=o_, in_=i_, axis=AX.X, op=ALU.add))(st16[:, 0, :], y3), reads=[ysb.b], writes=[st16.b])
                K.tt("pool", yfl.ap, ysb.ap, ysb.ap, ALU.mult, R=[ysb], W=[yfl])
                p.op("dve", (lambda o_, i_: lambda e: e.tensor_reduce(out=o_, in_=i_, axis=AX.X, op=ALU.add))(st16[:, 1, :], yfl.ap.rearrange("p (h n) -> p h n", h=16)),
                     reads=[yfl.b], writes=[st16.b])
                K.ts("dve", st16[:, 2, :], st16[:, 0, :], 1.0 / 64, None, ALU.mult, R=[st16], W=[st16])
                K.tt("dve", st16[:, 3, :], st16[:, 2, :], st16[:, 2, :], ALU.mult, R=[st16], W=[st16])
                K.stt(st16[:, 4, :], st16[:, 1, :], 1.0 / 64, st16[:, 3, :], ALU.mult, ALU.subtract, R=[st16], W=[st16])
                K.rsqrt(st16[:, 4, :], st16[:, 4, :], 1.0, 64e-5, R=[st16], W=[st16])
                K.tt("dve", y3, y3, bc(st16[:, 2, :].unsqueeze(2), [128, 16, 64]), ALU.subtract, R=[ysb, st16], W=[ysb])
                K.tt("dve", y3, y3, bc(st16[:, 4, :].unsqueeze(2), [128, 16, 64]), ALU.mult, R=[ysb, st16], W=[ysb])
                K.tt("pool", ysb.ap, ysb.ap, lnw.ap, ALU.mult, R=[ysb, lnw], W=[ysb])
                K.tt("pool", ysb.ap, ysb.ap, lnb.ap, ALU.add, R=[ysb, lnb], W=[ysb])
                K.tt("dve", yfl.ap.rearrange("p (h n) -> p h n", h=16), Vtok32.ap.rearrange("p (h n) -> p h n", h=16),
                     bc(st5[ci % 2].ap.unsqueeze(2), [128, 16, 64]), ALU.mult, R=[Vtok32, st5[ci % 2]], W=[yfl])
                K.tt("pool", ysb.ap, ysb.ap, yfl.ap, ALU.add, R=[ysb, yfl], W=[ysb])
                for hb in range(2):
                    psg = K.ps()
                    K.mm(psg.ap, gdb.ap, g2b[:, hb * 512:(hb + 1) * 512], R=[gdb, g2b], W=[psg])
                    K.tt("dve", ob16[:, hb * 512:(hb + 1) * 512], ysb[:, hb * 512:(hb + 1) * 512], psg.ap, ALU.mult, R=[ysb, psg], W=[ob16])
                psT = K.ps()
                psTb = psT.ap.bitcast(BF16)
                for cc in range(8):
                    K.tr(psTb[:, cc * 128:(cc + 1) * 128], ob16[:, cc * 128:(cc + 1) * 128], identb.ap, R=[ob16, identb], W=[psT])
                K.cp("act", mixo.ap, psTb.rearrange("p (a b) -> p a b", a=8), R=[psT], W=[mixo])
                K.dma("sp", MIXT[0:1024, tk:tk + 128].rearrange("(cc p) t -> p cc t", p=128), mixo.ap, R=[mixo], W=[MIXT])

        if stop == "B":
            return fin()
        K.reset()
        ahb = K.carve("ahb", [128, 32], F32); dsk = K.carve("dsk", [128, 16], F32); nwb = K.carve("nwb", [128, 1024], F32)
        K.dma("sp", ahb.ap, I["ssm_a_log"][0].rearrange("d h -> (d h)").partition_broadcast(128), W=[ahb])
        K.act(ahb.ap, ahb.ap, AF.Exp, R=[ahb], W=[ahb])
        K.ts("dve", ahb.ap, ahb.ap, -1.0, None, ALU.mult, R=[ahb], W=[ahb])
        K.dma("sp", dsk.ap, I["ssm_d"][0].partition_broadcast(128), W=[dsk])
        K.dma("sp", nwb.ap, I["ssm_norm_w"][0].partition_broadcast(128), W=[nwb])
        xsT = K.carve("xsT", [128, 8, 128], F32); bct = K.carve("bct", [128, 4, 128], F32); bcb = K.carve("bcb", [128, 4, 128], BF16)
        dtT = K.carve("dtT", [16, 128], F32); zsT = K.carve("zsT", [128, 8, 128], F32)
        xst = K.carve("xst", [128, 1024], F32); dtt = K.carve("dtt", [128, 16], F32); adt = K.carve("adt", [128, 16], F32)
        Btok = K.carve("Btok", [128, 2, 128], BF16)
        sm = K.carve("sm", [128, 4, 16], F32)
        rseg = K.carve("rseg", [128, 16, 128], F32); Eh = K.carve("Eh", [128, 16, 128], F32); Mh = K.carve("Mh", [128, 16, 128], BF16)
        CBm = K.carve("CBm", [128, 2, 128], F32)
        xd = K.carve("xd", [128, 1024], BF16); xdd = K.carve("xdd", [128, 1024], BF16); xdf = K.carve("xdf", [128, 1024], F32)
        SS32 = K.carve("SS32", [128, 16, 64], F32); SSbf = K.carve("SSbf", [128, 16, 64], BF16)
        yo = K.carve("yo", [128, 1024], F32); ysd = K.carve("ysd", [128, 1024], F32); yf2 = K.carve("yf2", [128, 1024], F32)
        zst = K.carve("zst", [128, 1024], F32); ss1 = K.carve("ss1", [128, 2], F32)
        ob2 = K.carve("ob2", [128, 1024], BF16); mixo2 = K.carve("mixo2", [128, 8, 128], BF16)
        for d in range(2):
            mA, mB, mC, mD = (UTi, UTi, LTs, UTi) if d == 0 else (LTi, LTi, UTs, LTi)
            K.memset("pool", SS32.ap, 0.0, W=[SS32])
            K.memset("pool", SSbf.ap, 0.0, W=[SSbf])
            order = list(range(NCH)) if d == 0 else list(range(NCH - 1, -1, -1))
            for ci, c in enumerate(order):
                tk = c * 128
                if ci == NCH // 2:
                    K.ts("dve", SS32.ap, SS32.ap, flag[:, 0:1], None, ALU.mult, R=[SS32, flag], W=[SS32])
                    K.cp("act", SSbf.ap, SS32.ap, R=[SS32], W=[SSbf])
                K.dma("sp", xsT.ap, PT_r[ROW_XS:ROW_XS + 1024, tk:tk + 128].rearrange("(cc p) t -> p cc t", p=128), R=[PT], W=[xsT])
                K.dma("sp", bct.ap, PT_r[ROW_B:ROW_B + 512, tk:tk + 128].rearrange("(cc p) t -> p cc t", p=128), R=[PT], W=[bct])
                K.dma("sp", dtT.ap, PT_r[ROW_DT + d * 16:ROW_DT + d * 16 + 16, tk:tk + 128], R=[PT], W=[dtT])
                K.cp("act", bcb.ap, bct.ap, R=[bct], W=[bcb])
                pss = [K.ps(), K.ps()]
                for cc in range(8):
                    K.tr(pss[cc // 4][:, (cc % 4) * 128:(cc % 4 + 1) * 128], xsT[:, cc, :], identf.ap, R=[xsT, identf], W=[pss[cc // 4]])
                for hb in range(2):
                    K.cp("act" if hb else "dve", xst[:, hb * 512:(hb + 1) * 512], pss[hb].ap, R=[pss[hb]], W=[xst])
                ps1 = K.ps()
                K.tr(ps1[:, 0:16], dtT.ap, identf[0:16, 0:16], R=[dtT, identf], W=[ps1])
                for g in range(2):
                    K.tr(ps1[:, 128 + g * 128:256 + g * 128], bct[:, g, :], identf.ap, R=[bct, identf], W=[ps1])
                K.cp("act", dtt.ap, ps1[:, 0:16], R=[ps1], W=[dtt])
                K.cp("act", Btok.ap, ps1[:, 128:384].rearrange("p (a b) -> p a b", a=2), R=[ps1], W=[Btok])
                K.tt("dve", adt.ap, dtt.ap, ahb[:, d * 16:(d + 1) * 16], ALU.mult, R=[dtt, ahb], W=[adt])
                ps2 = K.ps()
                K.mm(ps2[:, 0:16], mA.ap, adt.ap, R=[mA, adt], W=[ps2])
                K.mm(ps2[:, 16:32], onesf.ap, adt.ap, R=[onesf, adt], W=[ps2])
                K.cp("dve", sm[:, 0, :], ps2[:, 0:16], R=[ps2], W=[sm])
                K.cp("dve", sm[:, 3, :], ps2[:, 16:32], R=[ps2], W=[sm])
                K.act(sm[:, 1, :], sm[:, 0, :], AF.Exp, R=[sm], W=[sm])
                K.tt("dve", sm[:, 2, :], sm[:, 3, :], sm[:, 0, :], ALU.subtract, R=[sm], W=[sm])
                K.act(sm[:, 2, :], sm[:, 2, :], AF.Exp, R=[sm], W=[sm])
                K.act(sm[:, 3, :], sm[:, 3, :], AF.Exp, R=[sm], W=[sm])
                K.tt("pool", rseg.ap, bc(mB.ap.unsqueeze(1), [128, 16, 128]), bc(adt.ap.unsqueeze(2), [128, 16, 128]), ALU.mult, R=[mB, adt], W=[rseg])
                psc = K.ps()
                for g in range(2):
                    K.mm(psc[:, g * 128:(g + 1) * 128], bcb[:, g, :], bcb[:, 2 + g, :], R=[bcb], W=[psc])
                K.tt("dve", CBm.ap, psc[:, 0:256].rearrange("p (a b) -> p a b", a=2), bc(mD.ap.unsqueeze(1), [128, 2, 128]), ALU.mult, R=[psc, mD], W=[CBm])
                for hq in range(4):
                    pse = K.ps()
                    for hh in range(4):
                        K.mm(pse[:, hh * 128:(hh + 1) * 128], mC.ap, rseg[:, hq * 4 + hh, :], R=[mC, rseg], W=[pse])
                    K.act(Eh[:, hq * 4:(hq + 1) * 4, :], pse.ap.rearrange("p (a b) -> p a b", a=4), AF.Exp, R=[pse], W=[Eh])
                    K.tt("dve", Mh[:, hq * 4:(hq + 1) * 4, :], Eh[:, hq * 4:(hq + 1) * 4, :], bc(CBm[:, hq // 2, :].unsqueeze(1), [128, 4, 128]),
                         ALU.mult, R=[Eh, CBm], W=[Mh])
                x3 = xst.ap.rearrange("p (h n) -> p h n", h=16)
                K.tt("pool", xdf.ap.rearrange("p (h n) -> p h n", h=16), x3, bc(dtt.ap.unsqueeze(2), [128, 16, 64]), ALU.mult, R=[xst, dtt], W=[xdf])
                K.cp("act", xd.ap, xdf.ap, R=[xdf], W=[xd])
                K.tt("pool", xdd.ap.rearrange("p (h n) -> p h n", h=16), xdf.ap.rearrange("p (h n) -> p h n", h=16),
                     bc(sm[:, 2, :].unsqueeze(2), [128, 16, 64]), ALU.mult, R=[xdf, sm], W=[xdd])
                psd = [K.ps(), K.ps()]
                pso = [K.ps(), K.ps()]
                for h in range(16):
                    g = h // 8
                    K.mm(psd[h // 8][:, (h % 8) * 64:(h % 8 + 1) * 64], Mh[:, h, :], xd[:, h * 64:(h + 1) * 64], R=[Mh, xd], W=[psd[h // 8]])
                    K.mm(pso[h // 8][:, (h % 8) * 64:(h % 8 + 1) * 64], bcb[:, 2 + g, :], SSbf[:, h, :], R=[bcb, SSbf], W=[pso[h // 8]])
                for hb in range(2):
                    K.tt("dve", yo[:, hb * 512:(hb + 1) * 512].rearrange("p (h n) -> p h n", h=8), pso[hb].ap.rearrange("p (h n) -> p h n", h=8),
                         bc(sm[:, 1, hb * 8:(hb + 1) * 8].unsqueeze(2), [128, 8, 64]), ALU.mult, R=[pso[hb], sm], W=[yo])
                    K.tt("dve", ysd[:, hb * 512:(hb + 1) * 512], psd[hb].ap, yo[:, hb * 512:(hb + 1) * 512], ALU.add, R=[psd[hb], yo], W=[ysd])
                pst_ = [K.ps(), K.ps()]
                for h in range(16):
                    K.mm(pst_[h // 8][:, (h % 8) * 64:(h % 8 + 1) * 64], Btok[:, h // 8, :], xdd[:, h * 64:(h + 1) * 64], R=[Btok, xdd], W=[pst_[h // 8]])
                K.tt("dve", SS32.ap, SS32.ap, bc(sm[:, 3, :].unsqueeze(2), [128, 16, 64]), ALU.mult, R=[SS32, sm], W=[SS32])
                for hb in range(2):
                    K.tt("dve", SS32[:, hb * 8:(hb + 1) * 8, :], SS32[:, hb * 8:(hb + 1) * 8, :], pst_[hb].ap.rearrange("p (h n) -> p h n", h=8),
                         ALU.add, R=[pst_[hb], SS32], W=[SS32])
                K.cp("act", SSbf.ap, SS32.ap, R=[SS32], W=[SSbf])
                if d == 0:
                    K.dma("sp", YFS[tk:tk + 128, :], ysd.ap, R=[ysd], W=[YFS])
                    continue
                K.dma("sp", yf2.ap, YFS[tk:tk + 128, :], R=[YFS], W=[yf2])
                K.dma("sp", zsT.ap, PT_r[ROW_Z:ROW_Z + 1024, tk:tk + 128].rearrange("(cc p) t -> p cc t", p=128), R=[PT], W=[zsT])
                pss = [K.ps(), K.ps()]
                for cc in range(8):
                    K.tr(pss[cc // 4][:, (cc % 4) * 128:(cc % 4 + 1) * 128], zsT[:, cc, :], identf.ap, R=[zsT, identf], W=[pss[cc // 4]])
                for hb in range(2):
                    K.cp("act", zst[:, hb * 512:(hb + 1) * 512], pss[hb].ap, R=[pss[hb]], W=[zst])
                K.tt("pool", ysd.ap, ysd.ap, yf2.ap, ALU.add, R=[ysd, yf2], W=[ysd])
                K.tt("dve", yf2.ap.rearrange("p (h n) -> p h n", h=16), x3, bc(dsk.ap.unsqueeze(2), [128, 16, 64]), ALU.mult, R=[xst, dsk], W=[yf2])
                K.tt("pool", ysd.ap, ysd.ap, yf2.ap, ALU.add, R=[ysd, yf2], W=[ysd])
                K.tt("dve", ysd.ap, ysd.ap, zst.ap, ALU.mult, R=[ysd, zst], W=[ysd])
                K.act(yf2.ap, ysd.ap, AF.Square, accum=ss1[:, 0:1], R=[ysd], W=[yf2, ss1])
                K.rsqrt(ss1[:, 1:2], ss1[:, 0:1], 1.0 / 1024, 1e-5, R=[ss1], W=[ss1])
                K.stt(ob2.ap, ysd.ap, ss1[:, 1:2], nwb.ap, ALU.mult, ALU.mult, R=[ysd, ss1, nwb], W=[ob2])
                psT = K.ps()
                psTb = psT.ap.bitcast(BF16)
                for cc in range(8):
                    K.tr(psTb[:, cc * 128:(cc + 1) * 128], ob2[:, cc * 128:(cc + 1) * 128], identb.ap, R=[ob2, identb], W=[psT])
                K.cp("act", mixo2.ap, psTb.rearrange("p (a b) -> p a b", a=8), R=[psT], W=[mixo2])
                K.dma("sp", MIXT[1024:2048, tk:tk + 128].rearrange("(cc p) t -> p cc t", p=128), mixo2.ap, R=[mixo2], W=[MIXT])

        if stop == "C":
            return fin()
        K.reset()
        NT = min(512, HT)
        NB = NT // 128
        wout = K.carve("wout", [128, 16, D], BF16)

        woutb = [p.buf(f"woutb{i}") for i in range(4)]
        for q in range(4):
            p.op("sp", (lambda o_, i_: lambda e: e.dma_start(out=o_, in_=i_))(wout[:, :, q * 512:(q + 1) * 512], WOUTT[:, :, q * 512:(q + 1) * 512]),
                 reads=[WOUTT.b], writes=[woutb[q]], dma=True)
        xT = K.carve("xT", [128, 16, NT], F32)
        mixt = K.carve("mixt", [128, 16, NT], BF16)
        h2 = K.carve("h2", [128, 16, NT], BF16)
        xbl = [K.carve(f"xbl{i}", [128, D], F32) for i in range(2)]
        sqs = [K.carve(f"sq{i}", [128, NT], F32) for i in range(2)]
        rst = K.carve("rst", [128, NT], F32)

        def norm_stats(src, dstr):
            psn = K.ps()
            for k in range(16):
                sq = sqs[k % 2]
                K.tt("pool", sq.ap, src[:, k, :], src[:, k, :], ALU.mult, R=[src], W=[sq])
                K.mm(psn[:, 0:NT], onesf.ap, sq.ap, start=(k == 0), stop=(k == 15), R=[onesf, sq], W=[psn])
            K.rsqrt(dstr.ap, psn[:, 0:NT], 1.0 / D, 1e-6, R=[psn], W=[dstr])

        def load_xT(dst, t0):
            for b in range(NB):
                xb = xbl[b % 2]
                K.dma("sp", xb.ap, I["x"][t0 + b * 128:t0 + (b + 1) * 128, :], W=[xb])
                for q in range(4):
                    ps = K.ps()
                    for kk in range(4):
                        K.tr(ps[:, kk * 128:(kk + 1) * 128], xb[:, (q * 4 + kk) * 128:(q * 4 + kk + 1) * 128], identf.ap, R=[xb, identf], W=[ps])
                    K.cp("act" if q % 2 else "dve", dst[:, q * 4:(q + 1) * 4, b * 128:(b + 1) * 128], ps.ap.rearrange("p (a b) -> p a b", a=4), R=[ps], W=[dst])

        for ti in range(TT // NT):
            t0 = ti * NT
            hlf = t0 // HT
            load_xT(xT, t0)
            K.dma("sp", mixt.ap, MIXT[:, t0:t0 + NT].rearrange("(kc p) t -> p kc t", p=128), R=[MIXT], W=[mixt])
            for m in range(16):
                ps = K.ps()
                for k in range(16):
                    p.op("pe", (lambda o_, l_, r_, s0, s1_: lambda e: e.matmul(o_, lhsT=l_, rhs=r_, start=s0, stop=s1_))(
                        ps[:, 0:NT], wout[:, k, m * 128:(m + 1) * 128], mixt[:, k, :], k == 0, k == 15),
                        reads=[woutb[m // 4], mixt.b], writes=[ps.b])
                K.stt(xT[:, m, :], ps[:, 0:NT], modT[:, hlf, 2, m:m + 1], xT[:, m, :], ALU.mult, ALU.add, R=[ps, modT, xT], W=[xT])
            K.dma("sp", X1T[:, t0:t0 + NT].rearrange("(kc p) t -> p kc t", p=128), xT.ap, R=[xT], W=[X1T])
            norm_stats(xT, rst)
            for k in range(16):
                sq = sqs[k % 2]
                K.tt("dve", sq.ap, xT[:, k, :], rst.ap, ALU.mult, R=[xT, rst], W=[sq])
                K.act(h2[:, k, :], sq.ap, AF.Identity, bias=aff[:, 0, hlf, 3, k:k + 1], scale=aff[:, 0, hlf, 2, k:k + 1], R=[sq, aff], W=[h2])
            K.dma("sp", H2T[:, t0:t0 + NT].rearrange("(kc p) t -> p kc t", p=128), h2.ap, R=[h2], W=[H2T])

        if stop == "D1":
            return fin()
        K.reset()
        cf = K.carve("cf", [128, 88, 4], F32)
        fcw = I["ffn_conv_w"][0]
        for j in range(3):
            colvec(cf[:, :, j], fcw[j].rearrange("(c p) -> c p", p=128), 88, cf)
        colvec(cf[:, :, 3], I["ffn_conv_b"][0].rearrange("(c p) -> c p", p=128), 88, cf)
        h2t = K.carve("h2t", [128, 16, NT + 2], BF16)
        x1t = K.carve("x1t", [128, 16, NT], F32)
        aT = K.carve("aT", [128, 44, NT], BF16)
        wus = [K.carve(f"wu{i}", [128, 2, 16, 128], BF16) for i in range(4)]
        wds = [K.carve(f"wd{i}", [128, 44, 128], BF16) for i in range(4)]
        NS2 = NT // 256
        raw2 = [[K.carve(f"rawf{i}_{q}", [128, 258], F32) for q in range(2)] for i in range(2)]
        u2 = [[K.carve(f"u2_{i}_{q}", [128, 256], F32) for q in range(2)] for i in range(2)]
        s2b = [K.carve(f"s2b{i}", [128, 256], F32) for i in range(2)]
        sqs = [K.carve(f"sqd{i}", [128, NT], F32) for i in range(2)]
        rst = K.carve("rstd2", [128, NT], F32)
        yob = [K.carve(f"yob{i}", [128, D], F32) for i in range(1)]

        H2T_r = H2T.ap.rearrange("(kc p) t -> p kc t", p=128)
        ec = 0

        def ld_up(j):
            K.dma("act", wus[j % 4].ap, WUPT[j], R=[WUPT], W=[wus[j % 4]])

        def ld_dn(m):
            K.dma("act", wds[m % 4].ap, WDNT[m], R=[WDNT], W=[wds[m % 4]])

        for ti in range(TT // NT):
            t0 = ti * NT
            hlf = t0 // HT
            K.dma("sp", h2t[:, :, 1:NT + 1], H2T_r[:, :, t0:t0 + NT], R=[H2T], W=[h2t])
            if t0 == 0:
                K.memset("pool", h2t[:, :, 0:1], 0.0, W=[h2t])
            else:
                K.dma("sp", h2t[:, :, 0:1], H2T_r[:, :, t0 - 1:t0], R=[H2T], W=[h2t], slow=True)
                if t0 % HT == 0:
                    K.ts("dve", h2t[:, :, 0:1], h2t[:, :, 0:1], flag[:, 0:1], None, ALU.mult, R=[h2t, flag], W=[h2t])
            if t0 + NT == TT:
                K.memset("pool", h2t[:, :, NT + 1:NT + 2], 0.0, W=[h2t])
            else:
                K.dma("sp", h2t[:, :, NT + 1:NT + 2], H2T_r[:, :, t0 + NT:t0 + NT + 1], R=[H2T], W=[h2t], slow=True)
                if (t0 + NT) % HT == 0:
                    K.ts("dve", h2t[:, :, NT + 1:NT + 2], h2t[:, :, NT + 1:NT + 2], flag[:, 0:1], None, ALU.mult, R=[h2t, flag], W=[h2t])
            K.dma("sp", x1t.ap, X1T[:, t0:t0 + NT].rearrange("(kc p) t -> p kc t", p=128), R=[X1T], W=[x1t])
            if ti == 0:
                ld_up(0); ld_up(1); ld_up(2)
            for j in range(44):
                wu = wus[j % 4]
                jl = 0
                if j + 3 < 44:
                    ld_up(j + 3)
                else:
                    ld_dn(j + 3 - 44)
                for sub in range(NS2):
                    i = ec % 2
                    ec += 1
                    rw, uu, sb_ = raw2[i], u2[i], s2b[i]
                    pp = []
                    for part in range(2):
                        ps = K.ps()
                        pp.append(ps)
                        for k in range(16):
                            K.mm(ps[:, 0:258], wu[:, part, k, jl * 128:(jl + 1) * 128], h2t[:, k, sub * 256:sub * 256 + 258], start=(k == 0), stop=(k == 15), R=[wu, h2t], W=[ps])
                    for part in range(2):
                        K.cp("act", rw[part].ap, pp[part][:, 0:258], R=[pp[part]], W=[rw[part]])
                    cchs = [part * 44 + j for part in range(2)]
                    for part in range(2):
                        K.ts("dve", uu[part].ap, rw[part][:, 0:256], cf[:, cchs[part], 0:1], cf[:, cchs[part], 3:4], ALU.mult, ALU.add, R=[rw[part], cf], W=[uu[part]])
                    for tp in (1, 2):
                        for part in range(2):
                            K.stt(uu[part].ap, rw[part][:, tp:tp + 256], cf[:, cchs[part], tp:tp + 1], uu[part].ap, ALU.mult, ALU.add, R=[rw[part], cf, uu[part]], W=[uu[part]])
                    K.act(sb_.ap, uu[0].ap, AF.Silu, R=[uu[0]], W=[sb_])
                    K.tt("pool", aT[:, j, sub * 256:(sub + 1) * 256], sb_.ap, uu[1].ap, ALU.mult, R=[sb_, uu[1]], W=[aT])
            for m in range(16):
                wd_ = wds[m % 4]
                ml = 0
                if m + 3 < 16:
                    ld_dn(m + 3)
                elif ti + 1 < TT // NT:
                    ld_up(m + 3 - 16)
                ps = K.ps()
                for j in range(44):
                    K.mm(ps[:, 0:NT], wd_[:, j, ml * 128:(ml + 1) * 128], aT[:, j, :], start=(j == 0), stop=(j == 43), R=[wd_, aT], W=[ps])
                K.stt(x1t[:, m, :], ps[:, 0:NT], modT[:, hlf, 5, m:m + 1], x1t[:, m, :], ALU.mult, ALU.add, R=[ps, modT, x1t], W=[x1t])
            norm_stats(x1t, rst)
            for k in range(16):
                K.tt("dve", x1t[:, k, :], x1t[:, k, :], rst.ap, ALU.mult, R=[x1t, rst], W=[x1t])
                K.act(x1t[:, k, :], x1t[:, k, :], AF.Identity, bias=aff[:, 0, hlf, 5, k:k + 1], scale=aff[:, 0, hlf, 4, k:k + 1], R=[x1t, aff], W=[x1t])
            for b in range(NB):
                yo_ = yob[0]
                for q in range(4):
                    ps = K.ps()
                    for kk in range(4):
                        k = q * 4 + kk
                        K.tr(ps[:, kk * 128:(kk + 1) * 128], x1t[:, k, b * 128:(b + 1) * 128], identf.ap, R=[x1t, identf], W=[ps])
                    K.cp("act" if q % 2 else "dve", yo_[:, q * 512:(q + 1) * 512], ps.ap, R=[ps], W=[yo_])
                p.op("sp", (lambda o_, i_: lambda e: e.dma_start(out=o_, in_=i_))(y_out[t0 + b * 128:t0 + (b + 1) * 128, :], yo_.ap),
                     reads=[yo_.b], writes=[OUTB], dma=True)
        p.barrier()
        p.run(engsems, dmasems, block)
    return nc


_CACHE = {}


def _get_nc(TT):
    if TT not in _CACHE:
        _CACHE[TT] = build_program(TT)
    return _CACHE[TT]


WEIGHT_NAMES = ["norm1_w", "w_in", "rwkv_mu", "rwkv_w0", "rwkv_w2", "rwkv_a0", "rwkv_a2", "rwkv_g2", "rwkv_k_k", "rwkv_k_a",
                "rwkv_r_k", "rwkv_lnx_w", "rwkv_lnx_b", "ssm_conv_w", "ssm_conv_b", "ssm_dt_bias", "ssm_a_log", "ssm_d",
                "ssm_norm_w", "w_out", "norm2_w", "ffn_w_up", "ffn_conv_w", "ffn_conv_b", "ffn_w_down", "w_ada", "b_ada",
                "final_norm_w", "w_ada_final", "b_ada_final"]


def kernel(**inputs):
    xp = np.ascontiguousarray(inputs["x_prompt"], dtype=np.float32)
    xs = np.ascontiguousarray(inputs["x_sample"], dtype=np.float32)
    cp_ = np.asarray(inputs["c_prompt"], dtype=np.float32)
    cs_ = np.asarray(inputs["c_sample"], dtype=np.float32)
    Bp, Tp, _ = xp.shape
    Bs, Ts, _ = xs.shape
    TT = 2 * Tp
    assert Ts == TT and Bp == 2 * Bs and Bp // 2 + Bs == 8
    w = {k: np.ascontiguousarray(inputs[k], dtype=np.float32) for k in WEIGHT_NAMES}
    in_maps = []
    for c in range(8):
        m = dict(w)
        if c < Bp // 2:
            m["x"] = xp[2 * c:2 * c + 2].reshape(TT, D)
            m["c2"] = cp_[2 * c:2 * c + 2]
            m["flag"] = np.zeros((128, 1), np.float32)
        else:
            s = c - Bp // 2
            m["x"] = xs[s]
            m["c2"] = np.stack([cs_[s], cs_[s]])
            m["flag"] = np.ones((128, 1), np.float32)
        in_maps.append(m)
    nc = _get_nc(TT)
    res = run_bass_kernel_spmd(nc, in_maps, core_ids=list(range(8)))
    outs = [np.asarray(r["y"], dtype=np.float32) for r in res.results]
    y_prompt = np.stack([o.reshape(2, Tp, D) for o in outs[:Bp // 2]]).reshape(Bp, Tp, D)
    y_sample = np.stack(outs[Bp // 2:]).reshape(Bs, Ts, D)
    return (y_prompt, y_sample)
```

```python
import contextlib
import numpy as np
import concourse.bass as bass
import concourse.mybir as mybir
from concourse.bass_utils import run_bass_kernel_spmd

F32 = mybir.dt.float32
BF16 = mybir.dt.bfloat16
F32R = mybir.dt.float32r
import os
USE_F32R = False
AF = mybir.ActivationFunctionType
ALU = mybir.AluOpType
AX = mybir.AxisListType

ENGS = ("pe", "act", "dve", "pool", "sp")
NSLOT = 6
D = 2048
KC = 16
PIN = 6048
DFF = 5632
ROW_R, ROW_K, ROW_V, ROW_WD, ROW_AD, ROW_GD, ROW_Z, ROW_XS, ROW_B, ROW_C, ROW_DT = (
    0, 1024, 2048, 3072, 3200, 3328, 3456, 4480, 5504, 5760, 6016)
DEC_C = float(np.exp(-0.5))


class Buf:
    __slots__ = ("name", "ws", "rs", "multi")

    def __init__(self, name, multi=False):
        self.name = name
        self.ws = []
        self.rs = []
        self.multi = multi


class T:
    def __init__(self, ap, b):
        self.ap = ap
        self.b = b

    def __getitem__(self, k):
        return self.ap[k]


class Prog:
    def __init__(self, nc):
        self.nc = nc
        self.ops = []
        self.streams = {e: [] for e in ENGS}
        self.bufs = []

    def buf(self, name, multi=False):
        b = Buf(name, multi)
        self.bufs.append(b)
        return b

    def op(self, eng, fn, reads=(), writes=(), dma=False):
        oid = len(self.ops)
        deps = {}
        for b in reads:
            for w in b.ws:
                deps[w] = "raw"
        for b in writes:
            for r in b.rs:
                deps.setdefault(r, "war")
            if not b.multi:
                for w in b.ws:
                    deps.setdefault(w, "waw")
        self.ops.append(dict(eng=eng, fn=fn, deps=deps, dma=dma, sig=False))
        self.streams[eng].append(oid)
        for b in reads:
            b.rs.append(oid)
        for b in writes:
            if b.rs or not b.multi:
                b.ws = [oid]
                b.rs = []
            else:
                b.ws.append(oid)
        return oid

    def barrier(self):
        front = []
        for e in ENGS:
            lastc = None
            dm = []
            for oid in reversed(self.streams[e]):
                o = self.ops[oid]
                if o["fn"] is None:
                    continue
                if o["dma"]:
                    if len(dm) < NSLOT:
                        dm.append(oid)
                elif lastc is None:
                    lastc = oid
                if lastc is not None and len(dm) >= NSLOT:
                    break
            if lastc is not None:
                front.append(lastc)
            front += dm
        for e in ENGS:
            oid = len(self.ops)
            self.ops.append(dict(eng=e, fn=None, deps={f: "raw" for f in front}, dma=False, sig=False))
            self.streams[e].append(oid)
        for b in self.bufs:
            b.ws = []
            b.rs = []

    def run(self, engsems, dmasems, block):
        ops = self.ops
        for o in ops:
            e = o["eng"]
            nd = set()
            for d, kind in o["deps"].items():
                od = ops[d]
                if (not o["dma"]) and (not od["dma"]) and od["eng"] == e and o["fn"] is not None:
                    if e == "pe" or (kind != "raw" and e != "pool"):
                        continue
                nd.add(d)
            o["deps"] = nd
            for d in nd:
                ops[d]["sig"] = True
        cnt = {e: 0 for e in ENGS}
        dcnt = {}
        dslot = {e: 0 for e in ENGS}
        for e in ENGS:
            for oid in self.streams[e]:
                o = ops[oid]
                if o["fn"] is None:
                    continue
                if o["dma"]:
                    s = dslot[e] % NSLOT
                    dslot[e] += 1
                    prev = dcnt.get((e, s), 0)
                    dcnt[(e, s)] = prev + 16
                    o["sem"] = ("dma", e, s)
                    o["val"] = prev + 16
                    o["prev"] = prev
                elif o["sig"]:
                    cnt[e] += 1
                    o["sem"] = ("eng", e)
                    o["val"] = cnt[e]

        def semh(k):
            return engsems[k[1]] if k[0] == "eng" else dmasems[(k[1], k[2])]

        def make(e):
            def body(eng):
                known = {}
                for oid in self.streams[e]:
                    o = ops[oid]
                    need = {}
                    for d in o["deps"]:
                        od = ops[d]
                        k = od["sem"]
                        if od["val"] > need.get(k, 0):
                            need[k] = od["val"]
                    if o["dma"] and o["prev"] > 0:
                        k = o["sem"]
                        if o["prev"] > need.get(k, 0):
                            need[k] = o["prev"]
                    for k, v in need.items():
                        if known.get(k, 0) >= v:
                            continue
                        known[k] = v
                        eng.wait_ge(semh(k), v)
                    if o["fn"] is None:
                        continue
                    ins = o["fn"](eng)
                    if o["dma"]:
                        ins.then_inc(semh(o["sem"]), 16)
                    elif o["sig"]:
                        ins.then_inc(semh(o["sem"]), 1)
            return body

        names = {"pe": "tensor", "act": "scalar", "dve": "vector", "pool": "gpsimd", "sp": "sync"}
        for e in ENGS:
            if self.streams[e]:
                getattr(block, names[e])(make(e))


class KB:
    def __init__(self, nc, TT, es):
        self.nc = nc
        self.TT = TT
        self.HT = TT // 2
        self.es = es
        self.p = Prog(nc)
        self.psn = 0
        self.aoff = 0

    def sb(self, name, shape, dt):
        t = self.es.enter_context(self.nc.sbuf_tensor(name, list(shape), dt))
        return T(t[:] if len(shape) == 2 else t[tuple(slice(None) for _ in shape)], self.p.buf(name))

    def dram(self, name, shape, dt, multi=True):
        t = self.nc.dram_tensor(name, list(shape), dt, kind=("ExternalOutput" if self.dbg else "Internal")).ap()
        return T(t, self.p.buf(name, multi=multi))

    def init_arena(self, words):
        self.arena = self.es.enter_context(self.nc.sbuf_tensor("arena", [128, words], F32))
        self.awords = words
        self.pst = []
        for i in range(8):
            t = self.es.enter_context(self.nc.psum_tensor(f"psb{i}", [128, 512], F32))
            self.pst.append(T(t[:, :], self.p.buf(f"psb{i}")))

    def reset(self):
        self.p.barrier()
        self.aoff = 0

    def carve(self, name, shape, dt, parts=128):
        n = int(np.prod(shape[1:]))
        words = n if dt == F32 else (n + 1) // 2
        a = self.arena[0:shape[0], self.aoff:self.aoff + words]
        self.aoff += words
        assert self.aoff <= self.awords, (name, self.aoff, self.awords)
        if dt != F32:
            a = a.bitcast(dt)
            if n % 2:
                a = a[:, 0:n]
        if len(shape) == 3:
            a = a.rearrange("p (a b) -> p a b", a=shape[1])
        elif len(shape) == 4:
            a = a.rearrange("p (a b c) -> p a b c", a=shape[1], b=shape[2])
        return T(a, self.p.buf(name))

    def ps(self):
        t = self.pst[self.psn % 8]
        self.psn += 1
        return t

    def mm(self, out, lhsT, rhs, start=True, stop=True, R=(), W=()):
        return self.p.op("pe", lambda e: e.matmul(out, lhsT=lhsT, rhs=rhs, start=start, stop=stop),
                         reads=[t.b for t in R], writes=[t.b for t in W])

    def tr(self, out, in_, ident, R=(), W=()):
        return self.p.op("pe", lambda e: e.transpose(out, in_, ident), reads=[t.b for t in R], writes=[t.b for t in W])

    def act(self, out, in_, func, bias=0.0, scale=1.0, accum=None, R=(), W=()):
        if accum is None:
            f = lambda e: e.activation(out=out, in_=in_, func=func, bias=bias, scale=scale)
        else:
            f = lambda e: e.activation(out=out, in_=in_, func=func, bias=bias, scale=scale, accum_out=accum)
        return self.p.op("act", f, reads=[t.b for t in R], writes=[t.b for t in W])

    def tt(self, eng, out, a, b, op, R=(), W=()):
        return self.p.op(eng, lambda e: e.tensor_tensor(out=out, in0=a, in1=b, op=op),
                         reads=[t.b for t in R], writes=[t.b for t in W])

    def ts(self, eng, out, a, s1, s2, op0, op1=None, R=(), W=()):
        if op1 is None:
            f = lambda e: e.tensor_scalar(out=out, in0=a, scalar1=s1, scalar2=None, op0=op0)
        else:
            f = lambda e: e.tensor_scalar(out=out, in0=a, scalar1=s1, scalar2=s2, op0=op0, op1=op1)
        return self.p.op(eng, f, reads=[t.b for t in R], writes=[t.b for t in W])

    def stt(self, out, in0, scalar, in1, op0, op1, R=(), W=()):
        return self.p.op("dve", lambda e: e.scalar_tensor_tensor(out=out, in0=in0, scalar=scalar, in1=in1, op0=op0, op1=op1),
                         reads=[t.b for t in R], writes=[t.b for t in W])

    def rsqrt(self, out, in_, scale, eps, R=(), W=()):
        self.act(out, in_, AF.Sqrt, bias=eps, scale=scale, R=R, W=W)
        return self.p.op("dve", lambda e: e.reciprocal(out=out, in_=out), reads=[t.b for t in W], writes=[t.b for t in W])

    def cp(self, eng, out, in_, R=(), W=()):
        if eng == "act":
            return self.p.op("act", lambda e: e.copy(out=out, in_=in_), reads=[t.b for t in R], writes=[t.b for t in W])
        return self.p.op(eng, lambda e: e.tensor_copy(out=out, in_=in_), reads=[t.b for t in R], writes=[t.b for t in W])

    def memset(self, eng, out, val, W=()):
        return self.p.op(eng, lambda e: e.memset(out, val), writes=[t.b for t in W])

    def dma(self, q, out, in_, R=(), W=(), slow=False):
        if slow:
            f = lambda e: e.dma_start(out=out, in_=in_, allow_slow_non_contiguous=True)
        else:
            f = lambda e: e.dma_start(out=out, in_=in_)
        return self.p.op(q, f, reads=[t.b for t in R], writes=[t.b for t in W], dma=True)

    def aselect(self, out, in_, step, cm, cmp, fill, R=(), W=()):
        return self.p.op("pool", lambda e: e.affine_select(out=out, in_=in_, pattern=[[step, 128]], compare_op=cmp,
                                                           fill=fill, base=0, channel_multiplier=cm),
                         reads=[t.b for t in R], writes=[t.b for t in W])


def fr(ap):
    return ap.bitcast(F32R) if USE_F32R else ap


def bc(ap, shape):
    return ap.to_broadcast(list(shape))


def build_program(TT, stop=None, dbg=False):
    nc = bass.Bass("TRN2", target_bir_lowering=False)
    HT = TT // 2
    NCH = TT // 128
    I = {}

    def inp(name, shape):
        I[name] = nc.dram_tensor(name, list(shape), F32, kind="ExternalInput").ap()

    inp("x", [TT, D]); inp("c2", [2, D]); inp("flag", [128, 1])
    inp("norm1_w", [1, D]); inp("w_in", [1, D, PIN]); inp("rwkv_mu", [1, 2, 3456]); inp("rwkv_w0", [1, 2, 1024])
    inp("rwkv_w2", [1, 2, 64, 1024]); inp("rwkv_a0", [1, 2, 1024]); inp("rwkv_a2", [1, 2, 64, 1024])
    inp("rwkv_g2", [1, 128, 1024]); inp("rwkv_k_k", [1, 1024]); inp("rwkv_k_a", [1, 1024]); inp("rwkv_r_k", [1, 16, 64])
    inp("rwkv_lnx_w", [1, 1024]); inp("rwkv_lnx_b", [1, 1024]); inp("ssm_conv_w", [1, 7, 1536]); inp("ssm_conv_b", [1, 1536])
    inp("ssm_dt_bias", [1, 2, 16]); inp("ssm_a_log", [1, 2, 16]); inp("ssm_d", [1, 16]); inp("ssm_norm_w", [1, 1024])
    inp("w_out", [1, D, D]); inp("norm2_w", [1, D]); inp("ffn_w_up", [1, D, 2 * DFF]); inp("ffn_conv_w", [1, 3, 2 * DFF])
    inp("ffn_conv_b", [1, 2 * DFF]); inp("ffn_w_down", [1, DFF, D]); inp("w_ada", [1, D, 6 * D]); inp("b_ada", [1, 6 * D])
    inp("final_norm_w", [D]); inp("w_ada_final", [D, 2 * D]); inp("b_ada_final", [2 * D])
    y_out = nc.dram_tensor("y", [TT, D], F32, kind="ExternalOutput").ap()

    with contextlib.ExitStack() as es:
        K = KB(nc, TT, es)
        K.dbg = dbg
        p = K.p

        def fin():
            p.barrier()
            p.run(engsems, dmasems, block)
            return nc

        def gstep(g, n=1):
            if g is None:
                return
            for _ in range(n):
                try:
                    next(g)
                except StopIteration:
                    return

        def drain(g):
            if g is not None:
                for _ in g:
                    pass

        OUTB = p.buf("yout", multi=True)
        PT = K.dram("PT", [PIN, TT], F32)
        YFR = K.dram("YFR", [TT, 1024], F32)
        YFS = K.dram("YFS", [TT, 1024], F32)
        MIXT = K.dram("MIXT", [D, TT], BF16)
        X1T = K.dram("X1T", [D, TT], F32)
        H2T = K.dram("H2T", [D, TT], BF16)
        identf = K.sb("identf", [128, 128], F32)
        identb = K.sb("identb", [128, 128], BF16)
        onesf = K.sb("onesf", [128, 128], F32)
        blk1 = K.sb("blk1", [128, 128], F32)
        UTs = K.sb("UTs", [128, 128], F32); UTi = K.sb("UTi", [128, 128], F32)
        LTs = K.sb("LTs", [128, 128], F32); LTi = K.sb("LTi", [128, 128], F32)
        flag = K.sb("flagt", [128, 1], F32)
        modT = K.sb("modT", [128, 2, 8, 16], F32)
        aff = K.sb("aff", [128, 2, 2, 6, 16], F32)
        cw = K.sb("cw", [128, 48, 8], F32)
        cbF = K.sb("cbF", [128, NCH, 16], F32)
        K.init_arena(50400)
        engsems = {e: es.enter_context(nc.semaphore("s_" + e)) for e in ENGS}
        dmasems = {(e, s): es.enter_context(nc.semaphore(f"d_{e}_{s}")) for e in ("sp", "act", "pool") for s in range(NSLOT)}
        block = es.enter_context(nc.Block())

        K.memset("pool", identf.ap, 0.0, W=[identf])
        K.aselect(identf.ap, identf.ap, -1, 1, ALU.not_equal, 1.0, R=[identf], W=[identf])
        K.cp("pool", identb.ap, identf.ap, R=[identf], W=[identb])
        K.memset("pool", onesf.ap, 1.0, W=[onesf])
        K.memset("pool", blk1.ap, 0.0, W=[blk1])
        K.memset("pool", blk1[0:64, 0:64], 1.0, W=[blk1])
        K.memset("pool", blk1[64:128, 64:128], 1.0, W=[blk1])
        for (m, step, cm, cmp) in ((UTs, 1, -1, ALU.is_gt), (UTi, 1, -1, ALU.is_ge), (LTs, -1, 1, ALU.is_gt), (LTi, -1, 1, ALU.is_ge)):
            K.aselect(m.ap, onesf.ap, step, cm, cmp, 0.0, R=[onesf], W=[m])
        K.dma("sp", flag.ap, I["flag"], W=[flag])

        def colvec(dst, src2d, n, dstT):
            st = K.carve("cvst", [128, 128], F32)
            K.dma("sp", st[0:n, :], src2d, W=[st])
            ps = K.ps()
            K.tr(ps[:, 0:n], st[0:n, :], identf[0:n, 0:n], R=[st, identf], W=[ps])
            K.cp("dve", dst, ps[:, 0:n], R=[ps], W=[dstT])

        WADAT = K.dram("WADAT", [32, 128, 16, 512], BF16)
        WINT = K.dram("WINT", [48, 128, 16, 128], BF16)
        WOUTT = K.dram("WOUTT", [128, 16, D], BF16)
        WUPT = K.dram("WUPT", [44, 128, 2, 16, 128], BF16)
        WDNT = K.dram("WDNT", [16, 128, 44, 128], BF16)
        for kc in range(16):
            rs = slice(kc * 128, (kc + 1) * 128)
            K.dma("pool", WADAT[0:24, :, kc, :].rearrange("g p c -> p g c"), I["w_ada"][0][rs, :].rearrange("p (g c) -> p g c", c=512), W=[WADAT])
            K.dma("pool", WADAT[24:32, :, kc, :].rearrange("g p c -> p g c"), I["w_ada_final"][rs, :].rearrange("p (g c) -> p g c", c=512), W=[WADAT])
        zpad = K.carve("zpad", [128, 16, 96], BF16)
        K.memset("pool", zpad.ap, 0.0, W=[zpad])
        K.dma("pool", WINT[47, :, :, 32:128], zpad.ap, R=[zpad], W=[WINT])
        for kc in range(16):
            rs = slice(kc * 128, (kc + 1) * 128)
            K.dma("pool", WINT[0:47, :, kc, :].rearrange("g p c -> p g c"), I["w_in"][0][rs, 0:6016].rearrange("p (g c) -> p g c", c=128), W=[WINT])
            K.dma("pool", WINT[47, :, kc, 0:32], I["w_in"][0][rs, 6016:6048], W=[WINT])
        K.reset()
        c2t = K.carve("c2t", [2, D], F32)
        K.dma("sp", c2t.ap, I["c2"], W=[c2t])
        K.act(c2t.ap, c2t.ap, AF.Silu, R=[c2t], W=[c2t])
        scT = K.carve("scT", [128, 16, 2], BF16)
        ps = K.ps()
        for k in range(16):
            K.tr(ps[:, 2 * k:2 * k + 2], c2t[:, k * 128:(k + 1) * 128], identf[0:2, 0:2], R=[c2t, identf], W=[ps])
        K.cp("dve", scT.ap, ps[:, 0:32].rearrange("p (k h) -> p k h", h=2), R=[ps], W=[scT])
        badaT = K.carve("badaT", [128, 128], F32)
        colvec(badaT[:, 0:96], I["b_ada"][0].rearrange("(c p) -> c p", p=128), 96, badaT)
        colvec(badaT[:, 96:128], I["b_ada_final"].rearrange("(c p) -> c p", p=128), 32, badaT)
        wab = [K.carve(f"wab{i}", [128, 16, 512], BF16) for i in range(3)]

        nblk = 24 + 8
        def ld_ada(g):
            K.dma("act", wab[g % 3].ap, WADAT[g], R=[WADAT], W=[wab[g % 3]])
        ld_ada(0); ld_ada(1)
        for g in range(nblk):
            wt = wab[g % 3]
            if g + 2 < nblk:
                ld_ada(g + 2)
            if g % 4 == 0:
                ps = K.ps()
            for jl in range(4):
                j = (g % 4) * 4 + jl
                for k in range(16):
                    K.mm(ps[:, 2 * j:2 * j + 2], wt[:, k, jl * 128:(jl + 1) * 128], scT[:, k, :], start=(k == 0), stop=(k == 15),
                         R=[wt, scT], W=[ps])
            if g % 4 == 3:
                v = g // 4
                K.tt("dve", modT[:, :, v, :], ps[:, 0:32].rearrange("p (j h) -> p h j", h=2),
                     bc(badaT[:, v * 16:(v + 1) * 16].unsqueeze(1), [128, 2, 16]), ALU.add, R=[ps, badaT], W=[modT])
        nwT = K.carve("nwT", [128, 3, 16], F32)
        colvec(nwT[:, 0, :], I["norm1_w"][0].rearrange("(c p) -> c p", p=128), 16, nwT)
        colvec(nwT[:, 1, :], I["norm2_w"][0].rearrange("(c p) -> c p", p=128), 16, nwT)
        colvec(nwT[:, 2, :], I["final_norm_w"].rearrange("(c p) -> c p", p=128), 16, nwT)
        for hlf in range(2):
            for i, (vsc, vsh) in enumerate(((1, 0), (4, 3), (7, 6))):
                K.ts("dve", aff[:, 0, hlf, 2 * i, :], modT[:, hlf, vsc, :], 1.0, None, ALU.add, R=[modT], W=[aff])
                K.tt("dve", aff[:, 0, hlf, 2 * i, :], aff[:, 0, hlf, 2 * i, :], nwT[:, i, :], ALU.mult, R=[aff, nwT], W=[aff])
                K.cp("dve", aff[:, 0, hlf, 2 * i + 1, :], modT[:, hlf, vsh, :], R=[modT], W=[aff])
        K.ts("dve", aff[:, 1], aff[:, 0], flag[:, 0:1], None, ALU.mult, R=[aff, flag], W=[aff])

        K.memset("pool", cw.ap, 0.0, W=[cw])
        ps = K.ps()
        st = K.carve("cwst", [128, 128], F32)
        K.dma("sp", st[0:54, :], I["rwkv_mu"][0].rearrange("j (c p) -> (j c) p", p=128), W=[st])
        K.tr(ps[:, 0:54], st[0:54, :], identf[0:54, 0:54], R=[st, identf], W=[ps])
        K.cp("dve", cw[:, 0:27, 0], ps[:, 0:27], R=[ps], W=[cw])
        K.cp("dve", cw[:, 0:27, 2], ps[:, 27:54], R=[ps], W=[cw])
        K.tt("dve", cw[:, 0:27, 1], cw[:, 0:27, 0], cw[:, 0:27, 2], ALU.add, R=[cw], W=[cw])
        K.ts("dve", cw[:, 0:27, 1], cw[:, 0:27, 1], -1.0, 1.0, ALU.mult, ALU.add, R=[cw], W=[cw])
        ps = K.ps()
        st2 = K.carve("cwst2", [128, 128], F32)
        K.dma("sp", st2[0:84, :], I["ssm_conv_w"][0].rearrange("j (c p) -> (j c) p", p=128), W=[st2])
        K.tr(ps[:, 0:84], st2[0:84, :], identf[0:84, 0:84], R=[st2, identf], W=[ps])
        K.cp("dve", cw[:, 35:47, 0:7], ps[:, 0:84].rearrange("p (j c) -> p c j", c=12), R=[ps], W=[cw])
        colvec(cw[:, 35:47, 7], I["ssm_conv_b"][0].rearrange("(c p) -> c p", p=128), 12, cw)
        K.dma("sp", cw[0:32, 47, 7:8], I["ssm_dt_bias"].rearrange("a d h -> (d h) a"), W=[cw])

        if stop == "0":
            return fin()
        K.reset()
        ST = min(1024, HT)
        NSUB = ST // 256
        hTs = [K.carve(f"hT{i}", [128, 16, ST + 6], BF16) for i in range(2)]
        xts = [K.carve(f"xt{i}", [128, D], F32) for i in range(2)]
        junk = K.carve("junk", [128, D], BF16)
        st1 = [K.carve(f"st1_{i}", [128, 2], F32) for i in range(2)]
        winb = [K.carve(f"winb{i}", [128, 16, 128], BF16) for i in range(6)]
        s3 = list(range(27)); zz = list(range(27, 35)) + [47]; c7 = list(range(35, 47))
        sched = []
        for i_ in range(12):
            sched.append(c7[i_])
            sched += [s3.pop(0), s3.pop(0)]
            sched.append(zz.pop(0) if (i_ % 4 != 3 and zz) else (s3.pop(0) if s3 else zz.pop(0)))
        sched += s3 + zz
        assert sorted(sched) == list(range(48)), sched
        raws = [K.carve(f"raw{i}", [128, 262], F32) for i in range(3)]
        accs = [K.carve(f"acc{i}", [128, 256], F32) for i in range(3)]
        outs = [K.carve(f"outb{i}", [128, 256], F32) for i in range(3)]

        cnt = {"x": 0, "e": 0}

        def make_hT(dst, tok0, n, dcol, hlf, flagged, si, bi):
            i = cnt["x"] % 2
            cnt["x"] += 1
            xt, s1 = xts[i], st1[i]
            K.dma("sp", xt[0:n, :], I["x"][tok0:tok0 + n, :], W=[xt])
            K.act(junk[0:n, :], xt[0:n, :], AF.Square, accum=s1[0:n, 0:1], R=[xt], W=[junk, s1])
            K.rsqrt(s1[0:n, 1:2], s1[0:n, 0:1], 1.0 / D, 1e-6, R=[s1], W=[s1])
            K.act(xt[0:n, :], xt[0:n, :], AF.Identity, scale=s1[0:n, 1:2], R=[xt, s1], W=[xt])
            fl = 1 if flagged else 0
            for q in range(4):
                ps = K.ps()
                for kk in range(4):
                    k = q * 4 + kk
                    K.tr(ps[:, kk * 128:kk * 128 + n], xt[0:n, k * 128:(k + 1) * 128], identf[0:n, 0:n], R=[xt, identf], W=[ps])
                for kk in range(4):
                    k = q * 4 + kk
                    K.act(dst[:, k, dcol:dcol + n], ps[:, kk * 128:kk * 128 + n], AF.Identity,
                          bias=aff[:, fl, hlf, bi, k:k + 1], scale=aff[:, fl, hlf, si, k:k + 1], R=[ps, aff], W=[dst])

        def fill_halo(dst, t0, width, nh, si, bi):
            hlf = t0 // HT
            if t0 == 0:
                K.memset("pool", dst[:, :, 0:nh], 0.0, W=[dst])
            elif t0 % HT == 0:
                make_hT(dst, t0 - nh, nh, 0, hlf - 1, True, si, bi)
            else:
                make_hT(dst, t0 - nh, nh, 0, hlf, False, si, bi)
            tR = t0 + width
            if tR == TT:
                K.memset("pool", dst[:, :, nh + width:nh + width + nh], 0.0, W=[dst])
            elif tR % HT == 0:
                make_hT(dst, tR, nh, nh + width, hlf + 1, True, si, bi)
            else:
                make_hT(dst, tR, nh, nh + width, hlf, False, si, bi)

        def ld_win(n):
            n = n % 48
            K.dma("act", winb[wcnt["n"] % 6].ap, WINT[sched[n]], R=[WINT], W=[winb[wcnt["n"] % 6]])
            wcnt["n"] += 1
        wcnt = {"n": 0}

        def gen_hT(sti_):
            dst_ = hTs[sti_ % 2]
            t0_ = sti_ * ST
            for b in range(ST // 128):
                make_hT(dst_, t0_ + b * 128, 128, 3 + b * 128, t0_ // HT, False, 0, 1)
                yield
            fill_halo(dst_, t0_, ST, 3, 0, 1)
            yield

        drain(gen_hT(0))
        for sti in range(TT // ST):
            t0 = sti * ST
            hlf = t0 // HT
            hT = hTs[sti % 2]
            nxt_h = gen_hT(sti + 1) if sti + 1 < TT // ST else None
            if sti == 0:
                for n_ in range(4):
                    ld_win(n_)
            for si_, cc in enumerate(sched):
                wt = winb[(sti * 48 + si_) % 6]
                if si_ + 4 < 48 or sti + 1 < TT // ST:
                    ld_win(si_ + 4)
                if si_ % 4 == 0:
                    gstep(nxt_h)
                if True:
                    ccl = 0
                    M = 128 if cc < 47 else 32
                    for j in range(NSUB):
                        ps = K.ps()
                        for k in range(16):
                            K.mm(ps[0:M, 0:262], wt[:, k, ccl * 128:ccl * 128 + M], hT[:, k, j * 256:j * 256 + 262],
                                 start=(k == 0), stop=(k == 15), R=[wt, hT], W=[ps])
                        i = cnt["e"] % 3
                        cnt["e"] += 1
                        raw, acc, ob = raws[i], accs[i], outs[i]
                        if cc < 27:
                            K.cp("act", raw.ap, ps[:, 0:262], R=[ps], W=[raw])
                            K.ts("dve", ob.ap, raw[:, 2:258], cw[:, cc, 0:1], None, ALU.mult, R=[raw, cw], W=[ob])
                            K.stt(ob.ap, raw[:, 3:259], cw[:, cc, 1:2], ob.ap, ALU.mult, ALU.add, R=[raw, cw, ob], W=[ob])
                            K.stt(ob.ap, raw[:, 4:260], cw[:, cc, 2:3], ob.ap, ALU.mult, ALU.add, R=[raw, cw, ob], W=[ob])
                        elif cc < 35:
                            K.act(ob.ap, ps[:, 3:259], AF.Silu, R=[ps], W=[ob])
                        elif cc < 47:
                            K.cp("act", raw.ap, ps[:, 0:262], R=[ps], W=[raw])
                            K.ts("dve", acc.ap, raw[:, 0:256], cw[:, cc, 0:1], cw[:, cc, 7:8], ALU.mult, ALU.add, R=[raw, cw], W=[acc])
                            for tp in range(1, 7):
                                K.stt(acc.ap, raw[:, tp:tp + 256], cw[:, cc, tp:tp + 1], acc.ap, ALU.mult, ALU.add, R=[raw, cw, acc], W=[acc])
                            K.act(ob.ap, acc.ap, AF.Silu, R=[acc], W=[ob])
                        else:
                            K.act(acc[0:32, :], ps[0:32, 3:259], AF.Exp, bias=cw[0:32, 47, 7:8], R=[ps, cw], W=[acc])
                            K.act(ob[0:32, :], acc[0:32, :], AF.Ln, bias=1.0, R=[acc], W=[ob])
                        K.dma("sp", PT[cc * 128:cc * 128 + M, t0 + j * 256:t0 + (j + 1) * 256], ob[0:M, :], R=[ob], W=[PT])
            drain(nxt_h)

        if stop == "A":
            return fin()
        K.reset()
        for kc in range(16):
            rs = slice(kc * 128, (kc + 1) * 128)
            K.dma("pool", WOUTT[:, kc, :], I["w_out"][0][rs, :], W=[WOUTT])
        for kc in range(16):
            rs = slice(kc * 128, (kc + 1) * 128)
            for part in range(2):
                K.dma("pool", WUPT[:, :, part, kc, :].rearrange("j p c -> p j c"),
                      I["ffn_w_up"][0][rs, part * DFF:(part + 1) * DFF].rearrange("p (j c) -> p j c", c=128), W=[WUPT])
        for j in range(44):
            K.dma("pool", WDNT[:, :, j, :].rearrange("m p c -> p m c"), I["ffn_w_down"][0][j * 128:(j + 1) * 128, :].rearrange("p (m c) -> p m c", c=128), W=[WDNT])
        kkc = K.carve("kkc", [128, 8], F32); kac = K.carve("kac", [128, 8], F32); kam1 = K.carve("kam1", [128, 8], F32)
        w0c = K.carve("w0c", [128, 16], F32); a0c = K.carve("a0c", [128, 16], F32)
        colvec(kkc.ap, I["rwkv_k_k"][0].rearrange("(c p) -> c p", p=128), 8, kkc)
        colvec(kac.ap, I["rwkv_k_a"][0].rearrange("(c p) -> c p", p=128), 8, kac)
        K.ts("dve", kam1.ap, kac.ap, -1.0, None, ALU.add, R=[kac], W=[kam1])
        colvec(w0c.ap, I["rwkv_w0"][0].rearrange("d (c p) -> (d c) p", p=128), 16, w0c)
        colvec(a0c.ap, I["rwkv_a0"][0].rearrange("d (c p) -> (d c) p", p=128), 16, a0c)
        nbias = K.carve("nbias", [128, 32], F32)
        K.ts("dve", nbias[:, 0:16], w0c.ap, -1.0, None, ALU.mult, R=[w0c], W=[nbias])
        K.ts("dve", nbias[:, 16:32], a0c.ap, -1.0, None, ALU.mult, R=[a0c], W=[nbias])
        w2b = K.carve("w2b", [64, 2, 1024], BF16); a2b = K.carve("a2b", [64, 2, 1024], BF16); g2b = K.carve("g2b", [128, 1024], BF16)
        K.dma("pool", w2b.ap, I["rwkv_w2"][0].rearrange("d r c -> r d c"), W=[w2b])
        K.dma("pool", a2b.ap, I["rwkv_a2"][0].rearrange("d r c -> r d c"), W=[a2b])
        K.dma("pool", g2b.ap, I["rwkv_g2"][0], W=[g2b])
        rkblk = K.carve("rkblk", [128, 8, 2], F32)
        K.memset("pool", rkblk.ap, 0.0, W=[rkblk])
        rk_r = I["rwkv_r_k"][0].rearrange("(cc j) n -> j n cc", j=2)
        for j in range(2):
            K.dma("sp", rkblk[64 * j:64 * j + 64, :, j], rk_r[j], W=[rkblk], slow=True)
        lnw = K.carve("lnw", [128, 1024], F32); lnb = K.carve("lnb", [128, 1024], F32)
        K.dma("sp", lnw.ap, I["rwkv_lnx_w"][0].partition_broadcast(128), W=[lnw])
        K.dma("sp", lnb.ap, I["rwkv_lnx_b"][0].partition_broadcast(128), W=[lnb])
        rmask = K.carve("rmask", [128, 8, 128], F32)
        K.memset("pool", rmask.ap, 1.0, W=[rmask])
        K.memset("pool", rmask[:, :, 0:1], 0.0, W=[rmask])
        M4 = [K.carve(f"M4_{d}", [128, 4, 128], F32) for d in range(2)]
        for d, (ms, mi) in enumerate(((UTs, UTi), (LTs, LTi))):
            for q, m in enumerate((ms, mi, ms, mi)):
                K.cp("pool", M4[d][:, q, :], m.ap, R=[m], W=[M4[d]])
        MQ = [LTs, UTs]
        S32 = K.carve("S32", [128, 8, 64], F32); Sbf = K.carve("Sbf", [128, 8, 64], BF16)
        rkv = K.carve("rkv", [128, 3, 8, 128], F32)
        wdad = K.carve("wdad", [64, 2, 128], F32); wdadb = K.carve("wdadb", [64, 2, 128], BF16)
        gdt = K.carve("gdt", [128, 128], F32); gdb = K.carve("gdb", [128, 128], BF16)
        off_sg = K.aoff
        sg = K.carve("sg", [128, 8, 128], F32); ic = K.carve("ic", [128, 8, 128], F32)
        cum = K.carve("cum", [128, 8, 128], F32); cumx = K.carve("cumx", [128, 8, 128], F32)
        Q1v = T(K.arena[:, off_sg:off_sg + 2048].rearrange("p (a b) -> p a b", a=16), sg.b)
        QT1v = T(K.arena[:, off_sg + 2048:off_sg + 4096].rearrange("p (a b) -> p a b", a=16), cum.b)
        Wt = K.carve("Wt", [128, 8, 128], F32); Winv = K.carve("Winv", [128, 8, 128], F32); Wx = K.carve("Wx", [128, 8, 128], F32)
        kr = K.carve("kr", [128, 8, 128], F32); tmpA = K.carve("tmpA", [128, 8, 128], F32); tmpB = K.carve("tmpB", [128, 8, 128], F32)
        kkn = K.carve("kkn", [128, 8, 128], F32); kd = K.carve("kd", [128, 8, 128], F32)
        ar = K.carve("ar", [128, 2, 8, 128], BF16); bk = K.carve("bk", [128, 2, 8, 128], BF16); btk = K.carve("btk", [128, 2, 8, 128], BF16)
        BKtok = K.carve("BKtok", [128, 2, 1024], BF16); Vtok = K.carve("Vtok", [128, 1024], BF16); Vtok32 = K.carve("Vtok32", [128, 1024], F32)
        AT = K.carve("AT", [128, 16, 512], BF16)
        Qa = [K.carve("Q0f", [128, 16, 128], F32), Q1v]
        QTa = [K.carve("QT0f", [128, 16, 128], F32), QT1v]
        P32g = [K.carve(f"P32_{i}", [128, 4, 128], F32) for i in range(4)]
        Pbf = K.carve("Pbf", [128, 16, 128], BF16)
        X1 = K.carve("X1", [128, 1024], BF16); SAb = K.carve("SAb", [128, 1024], BF16)
        ysb = K.carve("ysb", [128, 1024], F32); yfl = K.carve("yfl", [128, 1024], F32)
        st16 = K.carve("st16", [128, 6, 16], F32)
        tot8 = K.carve("tot8", [128, 8, 1], F32)
        cm8 = K.carve("cm8", [128, 8, 1], F32); cL8 = K.carve("cL8", [128, 8, 1], F32)
        T1 = K.carve("T1", [128, 8, 128], F32)
        arc = K.carve("arc", [128, 2, 8, 128], BF16)
        ob16 = K.carve("ob16", [128, 1024], BF16); mixo = K.carve("mixo", [128, 8, 128], BF16)
        PT_r = PT.ap
        WL8 = K.carve("WL8", [128, 8, 1], F32)
        st5 = [K.carve(f"st5_{i}", [128, 16], F32) for i in range(2)]
        Qf, QTf = Qa[0], QTa[0]
        Qg = [T(Qf[:, hq * 4:(hq + 1) * 4, :], p.buf(f"Qg{hq}")) for hq in range(4)]
        QTg = [T(QTf[:, hq * 4:(hq + 1) * 4, :], p.buf(f"QTg{hq}")) for hq in range(4)]

        def prepP(d, c, par):
            tk = c * 128
            for q, row in enumerate((ROW_R, ROW_K, ROW_V)):
                K.dma("sp", rkv[:, q], PT_r[row:row + 1024, tk:tk + 128].rearrange("(cc p) t -> p cc t", p=128), R=[PT], W=[rkv])
            K.dma("sp", wdad[:, 0, :], PT_r[ROW_WD + d * 64:ROW_WD + d * 64 + 64, tk:tk + 128], R=[PT], W=[wdad])
            K.dma("sp", wdad[:, 1, :], PT_r[ROW_AD + d * 64:ROW_AD + d * 64 + 64, tk:tk + 128], R=[PT], W=[wdad])
            r_, k_, v_ = rkv[:, 0], rkv[:, 1], rkv[:, 2]
            yield
            K.act(wdadb[:, 0, :], wdad[:, 0, :], AF.Tanh, R=[wdad], W=[wdadb])
            K.cp("act", wdadb[:, 1, :], wdad[:, 1, :], R=[wdad], W=[wdadb])
            K.tt("pool", kr.ap, k_, bc(kkc.ap.unsqueeze(2), [128, 8, 128]), ALU.mult, R=[rkv, kkc], W=[kr])
            yield
            K.tt("pool", tmpA.ap, kr.ap, kr.ap, ALU.mult, R=[kr], W=[tmpA])
            yield
            for (wsrc, xi, dstt) in ((w2b, 0, sg), (a2b, 1, ic)):
                pss = [K.ps(), K.ps()]
                for cc in range(8):
                    K.mm(pss[cc // 4][:, (cc % 4) * 128:(cc % 4 + 1) * 128], wsrc[:, d, cc * 128:(cc + 1) * 128], wdadb[:, xi, :],
                         R=[wsrc, wdadb], W=[pss[cc // 4]])
                yield
                bsrc = w0c if xi == 0 else a0c
                for cc in range(8):
                    K.act(dstt[:, cc, :], pss[cc // 4][:, (cc % 4) * 128:(cc % 4 + 1) * 128], AF.Sigmoid,
                          bias=bsrc[:, d * 8 + cc:d * 8 + cc + 1], R=[pss[cc // 4], bsrc], W=[dstt])
                    if cc % 4 == 3:
                        yield
            pss = [K.ps(), K.ps()]
            for cc in range(8):
                K.mm(pss[cc // 4][:, (cc % 4) * 128:(cc % 4 + 1) * 128], blk1.ap, tmpA[:, cc, :], R=[blk1, tmpA], W=[pss[cc // 4]])
            yield
            for hb in range(2):
                K.rsqrt(tmpB[:, hb * 4:(hb + 1) * 4, :], pss[hb].ap.rearrange("p (a b) -> p a b", a=4), 1.0, 1e-24, R=[pss[hb]], W=[tmpB])
                yield
            K.tt("dve", kkn.ap, kr.ap, tmpB.ap, ALU.mult, R=[kr, tmpB], W=[kkn])
            yield
            sgf = sg.ap.rearrange("p a b -> p (a b)")
            cumf = cum.ap.rearrange("p a b -> p (a b)")
            p.op("dve", (lambda o_, m_, s_: lambda e: e.tensor_tensor_scan(out=o_, data0=m_, data1=s_, initial=0.0, op0=ALU.mult, op1=ALU.add))(
                cumf, rmask.ap.rearrange("p a b -> p (a b)"), sgf), reads=[rmask.b, sg.b], writes=[cum.b])
            yield
            if d == 1:
                K.tt("dve", tmpA.ap, sg.ap, cum.ap, ALU.subtract, R=[sg, cum], W=[tmpA])
                K.cp("dve", tot8.ap, cum[:, :, 127:128], R=[cum], W=[tot8])
                yield
                K.tt("dve", cum.ap, tmpA.ap, bc(tot8.ap, [128, 8, 128]), ALU.add, R=[tmpA, tot8], W=[cum])
                yield
            K.tt("dve", cumx.ap, cum.ap, sg.ap, ALU.subtract, R=[cum, sg], W=[cumx])
            eidx = 127 if d == 0 else 0
            K.act(Wt.ap, cum.ap, AF.Exp, scale=-DEC_C, R=[cum], W=[Wt])
            yield
            K.act(Wx.ap, cumx.ap, AF.Exp, scale=-DEC_C, R=[cumx], W=[Wx])
            K.cp("dve", cm8.ap, cum[:, :, 64:65], R=[cum], W=[cm8])
            K.cp("dve", cL8.ap, cum[:, :, eidx:eidx + 1], R=[cum], W=[cL8])
            yield
            K.tt("dve", T1.ap, cum.ap, bc(cm8.ap, [128, 8, 128]), ALU.subtract, R=[cum, cm8], W=[T1])
            yield
            K.tt("dve", cumx.ap, cumx.ap, bc(cm8.ap, [128, 8, 128]), ALU.subtract, R=[cumx, cm8], W=[cumx])
            K.act(Winv.ap, T1.ap, AF.Exp, scale=DEC_C, R=[T1], W=[Winv])
            yield
            K.act(T1.ap, T1.ap, AF.Exp, scale=-DEC_C, R=[T1], W=[T1])
            K.act(cumx.ap, cumx.ap, AF.Exp, scale=-DEC_C, R=[cumx], W=[cumx])
            K.tt("dve", cum.ap, cum.ap, bc(cL8.ap, [128, 8, 128]), ALU.subtract, R=[cum, cL8], W=[cum])
            yield
            K.act(cum.ap, cum.ap, AF.Exp, scale=DEC_C, R=[cum], W=[cum])
            K.tt("pool", tmpA.ap, ic.ap, bc(kac.ap.unsqueeze(2), [128, 8, 128]), ALU.mult, R=[ic, kac], W=[tmpA])
            yield
            K.tt("pool", tmpA.ap, tmpA.ap, bc(kam1.ap.unsqueeze(2), [128, 8, 128]), ALU.subtract, R=[tmpA, kam1], W=[tmpA])
            yield
            K.tt("dve", kd.ap, k_, tmpA.ap, ALU.mult, R=[rkv, tmpA], W=[kd])
            yield
            K.tt("dve", tmpB.ap, kkn.ap, ic.ap, ALU.mult, R=[kkn, ic], W=[tmpB])
            K.tt("pool", tmpA.ap, r_, kd.ap, ALU.mult, R=[rkv, kd], W=[tmpA])
            yield
            psb = K.ps()
            for cc in range(8):
                K.mm(psb[:, 2 * cc:2 * cc + 2], tmpA[:, cc, :], rkblk[:, cc, :], R=[tmpA, rkblk], W=[psb])
            if d == 0:
                K.cp("act", cbF[:, c, :], psb[:, 0:16], R=[psb], W=[cbF])
            else:
                K.tt("dve", st5[par].ap, psb[:, 0:16], cbF[:, c, :], ALU.add, R=[psb, cbF], W=[st5[par]])
            yield

        for d in range(2):
            K.memset("pool", S32.ap, 0.0, W=[S32])
            K.memset("pool", Sbf.ap, 0.0, W=[Sbf])
            order = list(range(NCH)) if d == 0 else list(range(NCH - 1, -1, -1))
            drain(prepP(d, order[0], 0))
            for ci, c in enumerate(order):
                tk = c * 128
                eidx = 127 if d == 0 else 0
                r_, k_, v_ = rkv[:, 0], rkv[:, 1], rkv[:, 2]
                if ci == NCH // 2:
                    K.ts("dve", S32.ap, S32.ap, flag[:, 0:1], None, ALU.mult, R=[S32, flag], W=[S32])
                    K.cp("act", Sbf.ap, S32.ap, R=[S32], W=[Sbf])
                K.cp("dve", WL8.ap, Wt[:, :, eidx:eidx + 1], R=[Wt], W=[WL8])
                K.stt(ar[:, 0], kkn.ap, -1.0, Wx.ap, ALU.mult, ALU.mult, R=[kkn, Wx], W=[ar])
                K.tt("pool", ar[:, 1], r_, Wt.ap, ALU.mult, R=[rkv, Wt], W=[ar])
                K.stt(arc[:, 0], kkn.ap, -1.0, cumx.ap, ALU.mult, ALU.mult, R=[kkn, cumx], W=[arc])
                K.tt("pool", arc[:, 1], r_, T1.ap, ALU.mult, R=[rkv, T1], W=[arc])
                K.tt("dve", bk[:, 0], tmpB.ap, Winv.ap, ALU.mult, R=[tmpB, Winv], W=[bk])
                K.tt("pool", bk[:, 1], kd.ap, Winv.ap, ALU.mult, R=[kd, Winv], W=[bk])
                K.tt("dve", btk[:, 0], tmpB.ap, cum.ap, ALU.mult, R=[tmpB, cum], W=[btk])
                K.tt("pool", btk[:, 1], kd.ap, cum.ap, ALU.mult, R=[kd, cum], W=[btk])
                for x_ in range(2):
                    psT = K.ps()
                    psTb = psT.ap.bitcast(BF16)
                    for cc in range(8):
                        K.tr(psTb[:, cc * 128:(cc + 1) * 128], btk[:, x_, cc, :], identb.ap, R=[btk, identb], W=[psT])
                    K.cp("act" if x_ == 0 else "dve", BKtok[:, x_, :], psTb, R=[psT], W=[BKtok])
                pss = [K.ps(), K.ps()]
                for cc in range(8):
                    K.tr(pss[cc // 4][:, (cc % 4) * 128:(cc % 4 + 1) * 128], rkv[:, 2, cc, :], identf.ap, R=[rkv, identf], W=[pss[cc // 4]])
                for hb in range(2):
                    K.cp("act", Vtok[:, hb * 512:(hb + 1) * 512], pss[hb].ap, R=[pss[hb]], W=[Vtok])
                    if d == 1:
                        K.cp("act", Vtok32[:, hb * 512:(hb + 1) * 512], pss[hb].ap, R=[pss[hb]], W=[Vtok32])
                for h in range(16):
                    cc, pb = h // 2, 64 * (h % 2)
                    psH = K.ps()
                    rhs = arc[pb:pb + 64, :, cc, :]
                    K.mm(psH[:, 0:256], bk[pb:pb + 64, 0, cc, :], rhs, R=[bk, arc], W=[psH])
                    K.mm(psH[:, 256:512], bk[pb:pb + 64, 1, cc, :], rhs, R=[bk, arc], W=[psH])
                    K.tt("dve", AT[:, h, :], psH.ap, M4[d].ap.rearrange("p a b -> p (a b)"), ALU.mult, R=[psH, M4[d]], W=[AT])
                    K.tt("dve", fr(QTf[:, h, :]), psH[:, 0:128], M4[d][:, 0, :], ALU.mult, R=[psH, M4[d]], W=[QTg[h // 4]])
                Q4 = Qf.ap.rearrange("p (c j) t -> p c j t", j=2)
                for hq in range(4):
                    psQ = K.ps()
                    par, c0 = hq % 2, 4 * (hq // 2)
                    pb = 64 * par
                    for hh in range(4):
                        cc = c0 + hh
                        K.mm(psQ[:, hh * 128:(hh + 1) * 128], arc[pb:pb + 64, 0, cc, :], bk[pb:pb + 64, 0, cc, :], R=[arc, bk], W=[psQ])
                    K.tt("dve", fr(Q4[:, c0:c0 + 4, par, :]), psQ.ap.rearrange("p (a b) -> p a b", a=4),
                         bc(MQ[d].ap.unsqueeze(1), [128, 4, 128]), ALU.mult, R=[psQ, MQ[d]], W=[Qg[c0 // 2], Qg[c0 // 2 + 1]])
                for hq in range(4):
                    K.tt("dve", fr(P32g[hq].ap), QTg[hq].ap, bc(identf.ap.unsqueeze(1), [128, 4, 128]), ALU.add, R=[QTg[hq], identf], W=[P32g[hq]])
                nxt = prepP(d, order[ci + 1], (ci + 1) % 2) if ci + 1 < NCH else None
                for lvl in range(1, 7):
                    for hq in range(4):
                        psa = K.ps()
                        for hh in range(4):
                            K.mm(psa[:, hh * 128:(hh + 1) * 128], fr(QTg[hq][:, hh, :]), fr(Qg[hq][:, hh, :]), R=[QTg[hq], Qg[hq]], W=[psa])
                        if lvl < 6:
                            psb_ = K.ps()
                            for hh in range(4):
                                K.mm(psb_[:, hh * 128:(hh + 1) * 128], fr(Qg[hq][:, hh, :]), fr(QTg[hq][:, hh, :]), R=[QTg[hq], Qg[hq]], W=[psb_])
                        K.cp("act", fr(Qg[hq].ap), psa.ap.rearrange("p (a b) -> p a b", a=4), R=[psa], W=[Qg[hq]])
                        if lvl < 6:
                            K.cp("act", fr(QTg[hq].ap), psb_.ap.rearrange("p (a b) -> p a b", a=4), R=[psb_], W=[QTg[hq]])
                        gstep(nxt)
                    for hq in range(4):
                        psq = K.ps()
                        for hh in range(4):
                            K.mm(psq[:, hh * 128:(hh + 1) * 128], fr(Qg[hq][:, hh, :]), fr(P32g[hq][:, hh, :]), R=[Qg[hq], P32g[hq]], W=[psq])
                        K.tt("dve", fr(P32g[hq].ap), P32g[hq].ap, psq.ap.rearrange("p (a b) -> p a b", a=4),
                             ALU.add, R=[psq, P32g[hq]], W=[P32g[hq]])
                        gstep(nxt)
                for hq in range(4):
                    K.cp("pool", Pbf[:, hq * 4:(hq + 1) * 4, :], P32g[hq].ap, R=[P32g[hq]], W=[Pbf])
                drain(nxt)
                pss = [K.ps(), K.ps()]
                for h in range(16):
                    cc, pb = h // 2, 64 * (h % 2)
                    o_ = pss[h % 2][:, cc * 64:(cc + 1) * 64]
                    K.mm(o_, AT[:, h, 256:384], Vtok[:, h * 64:(h + 1) * 64], start=True, stop=False, R=[AT, Vtok], W=[pss[h % 2]])
                    K.mm(o_, ar[pb:pb + 64, 0, cc, :], Sbf[pb:pb + 64, cc, :], start=False, stop=True, R=[ar, Sbf], W=[pss[h % 2]])
                for hb in range(2):
                    K.cp("act", X1.ap.rearrange("p (c j n) -> p c j n", j=2, n=64)[:, :, hb, :], pss[hb].ap.rearrange("p (c n) -> p c n", n=64), R=[pss[hb]], W=[X1])
                pss = [K.ps(), K.ps()]
                for h in range(16):
                    K.mm(pss[h // 8][:, (h % 8) * 64:(h % 8 + 1) * 64], Pbf[:, h, :], X1[:, h * 64:(h + 1) * 64], R=[Pbf, X1], W=[pss[h // 8]])
                for hb in range(2):
                    K.cp("dve" if hb else "act", SAb[:, hb * 512:(hb + 1) * 512], pss[hb].ap, R=[pss[hb]], W=[SAb])
                pss = [K.ps(), K.ps()]
                for h in range(16):
                    cc, pb = h // 2, 64 * (h % 2)
                    o_ = pss[h % 2][:, cc * 64:(cc + 1) * 64]
                    K.mm(o_, ar[pb:pb + 64, 1, cc, :], Sbf[pb:pb + 64, cc, :], start=True, stop=False, R=[ar, Sbf], W=[pss[h % 2]])
                    K.mm(o_, AT[:, h, 128:256], SAb[:, h * 64:(h + 1) * 64], start=False, stop=False, R=[AT, SAb], W=[pss[h % 2]])
                    K.mm(o_, AT[:, h, 384:512], Vtok[:, h * 64:(h + 1) * 64], start=False, stop=True, R=[AT, Vtok], W=[pss[h % 2]])
                for hb in range(2):
                    K.cp("act" if hb else "dve", ysb.ap.rearrange("p (c j n) -> p c j n", j=2, n=64)[:, :, hb, :], pss[hb].ap.rearrange("p (c n) -> p c n", n=64), R=[pss[hb]], W=[ysb])
                pss = [K.ps(), K.ps()]
                for cc in range(8):
                    o_ = pss[cc // 4][:, (cc % 4) * 128:(cc % 4 + 1) * 128]
                    K.mm(o_, BKtok[:, 0, cc * 128:(cc + 1) * 128], SAb[:, cc * 128:(cc + 1) * 128], start=True, stop=False, R=[BKtok, SAb], W=[pss[cc // 4]])
                    K.mm(o_, BKtok[:, 1, cc * 128:(cc + 1) * 128], Vtok[:, cc * 128:(cc + 1) * 128], start=False, stop=True, R=[BKtok, Vtok], W=[pss[cc // 4]])
                K.tt("dve", S32.ap, S32.ap, bc(WL8.ap, [128, 8, 64]), ALU.mult, R=[S32, WL8], W=[S32])
                for hb in range(2):
                    v4 = pss[hb].ap.rearrange("p (a b) -> p a b", a=4)
                    K.tt("dve", S32[0:64, hb * 4:(hb + 1) * 4, :], S32[0:64, hb * 4:(hb + 1) * 4, :], v4[0:64, :, 0:64], ALU.add, R=[pss[hb], S32], W=[S32])
                    K.tt("dve", S32[64:128, hb * 4:(hb + 1) * 4, :], S32[64:128, hb * 4:(hb + 1) * 4, :], v4[64:128, :, 64:128], ALU.add, R=[pss[hb], S32], W=[S32])
                K.cp("act", Sbf.ap, S32.ap, R=[S32], W=[Sbf])
                if stop == "B5" or (d == 1 and stop == "B5b"):
                    return fin()
                if d == 0:
                    K.dma("sp", YFR[tk:tk + 128, :], ysb.ap, R=[ysb], W=[YFR])
                    continue
                if stop == "B7":
                    return fin()
                K.dma("sp", yfl.ap, YFR[tk:tk + 128, :], R=[YFR], W=[yfl])
                K.dma("sp", gdt.ap, PT_r[ROW_GD:ROW_GD + 128, tk:tk + 128], R=[PT], W=[gdt])
                K.act(gdb.ap, gdt.ap, AF.Sigmoid, R=[gdt], W=[gdb])
                K.tt("dve", ysb.ap, ysb.ap, yfl.ap, ALU.add, R=[ysb, yfl], W=[ysb])
                y3 = ysb.ap.rearrange("p (h n) -> p h n", h=16)
                p.op("dve", (lambda o_, i_: lambda e: e.tensor_reduce(out=o_, in_=i_, axis=AX.X, op=ALU.add))(st16[:, 0, :], y3), reads=[ysb.b], writes=[st16.b])
                K.tt("pool", yfl.ap, ysb.ap, ysb.ap, ALU.mult, R=[ysb], W=[yfl])
                p.op("dve", (lambda o_, i_: lambda e: e.tensor_reduce(out=o_, in_=i_, axis=AX.X, op=ALU.add))(st16[:, 1, :], yfl.ap.rearrange("p (h n) -> p h n", h=16)),
                     reads=[yfl.b], writes=[st16.b])
                K.ts("dve", st16[:, 2, :], st16[:, 0, :], 1.0 / 64, None, ALU.mult, R=[st16], W=[st16])
                K.tt("dve", st16[:, 3, :], st16[:, 2, :], st16[:, 2, :], ALU.mult, R=[st16], W=[st16])
                K.stt(st16[:, 4, :], st16[:, 1, :], 1.0 / 64, st16[:, 3, :], ALU.mult, ALU.subtract, R=[st16], W=[st16])
                K.rsqrt(st16[:, 4, :], st16[:, 4, :], 1.0, 64e-5, R=[st16], W=[st16])
                K.tt("dve", y3, y3, bc(st16[:, 2, :].unsqueeze(2), [128, 16, 64]), ALU.subtract, R=[ysb, st16], W=[ysb])
                K.tt("dve", y3, y3, bc(st16[:, 4, :].unsqueeze(2), [128, 16, 64]), ALU.mult, R=[ysb, st16], W=[ysb])
                K.tt("pool", ysb.ap, ysb.ap, lnw.ap, ALU.mult, R=[ysb, lnw], W=[ysb])
                K.tt("pool", ysb.ap, ysb.ap, lnb.ap, ALU.add, R=[ysb, lnb], W=[ysb])
                K.tt("dve", yfl.ap.rearrange("p (h n) -> p h n", h=16), Vtok32.ap.rearrange("p (h n) -> p h n", h=16),
                     bc(st5[ci % 2].ap.unsqueeze(2), [128, 16, 64]), ALU.mult, R=[Vtok32, st5[ci % 2]], W=[yfl])
                K.tt("pool", ysb.ap, ysb.ap, yfl.ap, ALU.add, R=[ysb, yfl], W=[ysb])
                for hb in range(2):
                    psg = K.ps()
                    K.mm(psg.ap, gdb.ap, g2b[:, hb * 512:(hb + 1) * 512], R=[gdb, g2b], W=[psg])
                    K.tt("dve", ob16[:, hb * 512:(hb + 1) * 512], ysb[:, hb * 512:(hb + 1) * 512], psg.ap, ALU.mult, R=[ysb, psg], W=[ob16])
                psT = K.ps()
                psTb = psT.ap.bitcast(BF16)
                for cc in range(8):
                    K.tr(psTb[:, cc * 128:(cc + 1) * 128], ob16[:, cc * 128:(cc + 1) * 128], identb.ap, R=[ob16, identb], W=[psT])
                K.cp("act", mixo.ap, psTb.rearrange("p (a b) -> p a b", a=8), R=[psT], W=[mixo])
                K.dma("sp", MIXT[0:1024, tk:tk + 128].rearrange("(cc p) t -> p cc t", p=128), mixo.ap, R=[mixo], W=[MIXT])

        if stop == "B":
            return fin()
        K.reset()
        ahb = K.carve("ahb", [128, 32], F32); dsk = K.carve("dsk", [128, 16], F32); nwb = K.carve("nwb", [128, 1024], F32)
        K.dma("sp", ahb.ap, I["ssm_a_log"][0].rearrange("d h -> (d h)").partition_broadcast(128), W=[ahb])
        K.act(ahb.ap, ahb.ap, AF.Exp, R=[ahb], W=[ahb])
        K.ts("dve", ahb.ap, ahb.ap, -1.0, None, ALU.mult, R=[ahb], W=[ahb])
        K.dma("sp", dsk.ap, I["ssm_d"][0].partition_broadcast(128), W=[dsk])
        K.dma("sp", nwb.ap, I["ssm_norm_w"][0].partition_broadcast(128), W=[nwb])
        SS32 = K.carve("SS32", [128, 16, 64], F32); SSbf = K.carve("SSbf", [128, 16, 64], BF16)
        csets = []
        for i_ in range(2):
            cs = {}
            for nm_, shp_, dt_ in (("xsT", [128, 8, 128], F32), ("bct", [128, 4, 128], F32), ("bcb", [128, 4, 128], BF16), ("dtT", [16, 128], F32),
                                   ("zsT", [128, 8, 128], F32), ("xst", [128, 1024], F32), ("dtt", [128, 16], F32), ("adt", [128, 16], F32),
                                   ("Btok", [128, 2, 128], BF16), ("sm", [128, 4, 16], F32), ("rseg", [128, 16, 128], F32), ("Eh", [128, 16, 128], F32),
                                   ("Mh", [128, 16, 128], BF16), ("CBm", [128, 2, 128], F32), ("xd", [128, 1024], BF16), ("xdd", [128, 1024], BF16),
                                   ("xdf", [128, 1024], F32), ("yo", [128, 1024], F32), ("ysd", [128, 1024], F32), ("yf2", [128, 1024], F32),
                                   ("zst", [128, 1024], F32), ("ss1", [128, 2], F32), ("ob2", [128, 1024], BF16), ("mixo2", [128, 8, 128], BF16)):
                cs[nm_] = K.carve(f"{nm_}_{i_}", shp_, dt_)
            csets.append(cs)
        for d in range(2):
            mA, mB, mC, mD = (UTi, UTi, LTs, UTi) if d == 0 else (LTi, LTi, UTs, LTi)
            K.memset("pool", SS32.ap, 0.0, W=[SS32])
            K.memset("pool", SSbf.ap, 0.0, W=[SSbf])
            order = list(range(NCH)) if d == 0 else list(range(NCH - 1, -1, -1))
            for ci, c in enumerate(order):
                tk = c * 128
                cs = csets[ci % 2]
                xsT, bct, bcb, dtT, zsT, xst, dtt, adt, Btok, sm, rseg, Eh, Mh, CBm, xd, xdd, xdf, yo, ysd, yf2, zst, ss1, ob2, mixo2 = (
                    cs[k_] for k_ in ("xsT", "bct", "bcb", "dtT", "zsT", "xst", "dtt", "adt", "Btok", "sm", "rseg", "Eh", "Mh", "CBm", "xd", "xdd",
                                      "xdf", "yo", "ysd", "yf2", "zst", "ss1", "ob2", "mixo2"))
                if ci == NCH // 2:
                    K.ts("dve", SS32.ap, SS32.ap, flag[:, 0:1], None, ALU.mult, R=[SS32, flag], W=[SS32])
                    K.cp("act", SSbf.ap, SS32.ap, R=[SS32], W=[SSbf])
                K.dma("sp", xsT.ap, PT_r[ROW_XS:ROW_XS + 1024, tk:tk + 128].rearrange("(cc p) t -> p cc t", p=128), R=[PT], W=[xsT])
                K.dma("sp", bct.ap, PT_r[ROW_B:ROW_B + 512, tk:tk + 128].rearrange("(cc p) t -> p cc t", p=128), R=[PT], W=[bct])
                K.dma("sp", dtT.ap, PT_r[ROW_DT + d * 16:ROW_DT + d * 16 + 16, tk:tk + 128], R=[PT], W=[dtT])
                K.cp("act", bcb.ap, bct.ap, R=[bct], W=[bcb])
                pss = [K.ps(), K.ps()]
                for cc in range(8):
                    K.tr(pss[cc // 4][:, (cc % 4) * 128:(cc % 4 + 1) * 128], xsT[:, cc, :], identf.ap, R=[xsT, identf], W=[pss[cc // 4]])
                for hb in range(2):
                    K.cp("act" if hb else "dve", xst[:, hb * 512:(hb + 1) * 512], pss[hb].ap, R=[pss[hb]], W=[xst])
                ps1 = K.ps()
                K.tr(ps1[:, 0:16], dtT.ap, identf[0:16, 0:16], R=[dtT, identf], W=[ps1])
                for g in range(2):
                    K.tr(ps1[:, 128 + g * 128:256 + g * 128], bct[:, g, :], identf.ap, R=[bct, identf], W=[ps1])
                K.cp("act", dtt.ap, ps1[:, 0:16], R=[ps1], W=[dtt])
                K.cp("act", Btok.ap, ps1[:, 128:384].rearrange("p (a b) -> p a b", a=2), R=[ps1], W=[Btok])
                K.tt("dve", adt.ap, dtt.ap, ahb[:, d * 16:(d + 1) * 16], ALU.mult, R=[dtt, ahb], W=[adt])
                ps2 = K.ps()
                K.mm(ps2[:, 0:16], mA.ap, adt.ap, R=[mA, adt], W=[ps2])
                K.mm(ps2[:, 16:32], onesf.ap, adt.ap, R=[onesf, adt], W=[ps2])
                K.cp("dve", sm[:, 0, :], ps2[:, 0:16], R=[ps2], W=[sm])
                K.cp("dve", sm[:, 3, :], ps2[:, 16:32], R=[ps2], W=[sm])
                K.act(sm[:, 1, :], sm[:, 0, :], AF.Exp, R=[sm], W=[sm])
                K.tt("dve", sm[:, 2, :], sm[:, 3, :], sm[:, 0, :], ALU.subtract, R=[sm], W=[sm])
                K.act(sm[:, 2, :], sm[:, 2, :], AF.Exp, R=[sm], W=[sm])
                K.act(sm[:, 3, :], sm[:, 3, :], AF.Exp, R=[sm], W=[sm])
                K.tt("pool", rseg.ap, bc(mB.ap.unsqueeze(1), [128, 16, 128]), bc(adt.ap.unsqueeze(2), [128, 16, 128]), ALU.mult, R=[mB, adt], W=[rseg])
                psc = K.ps()
                for g in range(2):
                    K.mm(psc[:, g * 128:(g + 1) * 128], bcb[:, g, :], bcb[:, 2 + g, :], R=[bcb], W=[psc])
                K.tt("dve", CBm.ap, psc[:, 0:256].rearrange("p (a b) -> p a b", a=2), bc(mD.ap.unsqueeze(1), [128, 2, 128]), ALU.mult, R=[psc, mD], W=[CBm])
                for hq in range(4):
                    pse = K.ps()
                    for hh in range(4):
                        K.mm(pse[:, hh * 128:(hh + 1) * 128], mC.ap, rseg[:, hq * 4 + hh, :], R=[mC, rseg], W=[pse])
                    K.act(Eh[:, hq * 4:(hq + 1) * 4, :], pse.ap.rearrange("p (a b) -> p a b", a=4), AF.Exp, R=[pse], W=[Eh])
                    K.tt("dve", Mh[:, hq * 4:(hq + 1) * 4, :], Eh[:, hq * 4:(hq + 1) * 4, :], bc(CBm[:, hq // 2, :].unsqueeze(1), [128, 4, 128]),
                         ALU.mult, R=[Eh, CBm], W=[Mh])
                x3 = xst.ap.rearrange("p (h n) -> p h n", h=16)
                K.tt("pool", xdf.ap.rearrange("p (h n) -> p h n", h=16), x3, bc(dtt.ap.unsqueeze(2), [128, 16, 64]), ALU.mult, R=[xst, dtt], W=[xdf])
                K.cp("act", xd.ap, xdf.ap, R=[xdf], W=[xd])
                K.tt("pool", xdd.ap.rearrange("p (h n) -> p h n", h=16), xdf.ap.rearrange("p (h n) -> p h n", h=16),
                     bc(sm[:, 2, :].unsqueeze(2), [128, 16, 64]), ALU.mult, R=[xdf, sm], W=[xdd])
                psd = [K.ps(), K.ps()]
                pso = [K.ps(), K.ps()]
                for h in range(16):
                    g = h // 8
                    K.mm(psd[h // 8][:, (h % 8) * 64:(h % 8 + 1) * 64], Mh[:, h, :], xd[:, h * 64:(h + 1) * 64], R=[Mh, xd], W=[psd[h // 8]])
                    K.mm(pso[h // 8][:, (h % 8) * 64:(h % 8 + 1) * 64], bcb[:, 2 + g, :], SSbf[:, h, :], R=[bcb, SSbf], W=[pso[h // 8]])
                for hb in range(2):
                    K.tt("dve", yo[:, hb * 512:(hb + 1) * 512].rearrange("p (h n) -> p h n", h=8), pso[hb].ap.rearrange("p (h n) -> p h n", h=8),
                         bc(sm[:, 1, hb * 8:(hb + 1) * 8].unsqueeze(2), [128, 8, 64]), ALU.mult, R=[pso[hb], sm], W=[yo])
                    K.tt("dve", ysd[:, hb * 512:(hb + 1) * 512], psd[hb].ap, yo[:, hb * 512:(hb + 1) * 512], ALU.add, R=[psd[hb], yo], W=[ysd])
                pst_ = [K.ps(), K.ps()]
                for h in range(16):
                    K.mm(pst_[h // 8][:, (h % 8) * 64:(h % 8 + 1) * 64], Btok[:, h // 8, :], xdd[:, h * 64:(h + 1) * 64], R=[Btok, xdd], W=[pst_[h // 8]])
                K.tt("dve", SS32.ap, SS32.ap, bc(sm[:, 3, :].unsqueeze(2), [128, 16, 64]), ALU.mult, R=[SS32, sm], W=[SS32])
                for hb in range(2):
                    K.tt("dve", SS32[:, hb * 8:(hb + 1) * 8, :], SS32[:, hb * 8:(hb + 1) * 8, :], pst_[hb].ap.rearrange("p (h n) -> p h n", h=8),
                         ALU.add, R=[pst_[hb], SS32], W=[SS32])
                K.cp("act", SSbf.ap, SS32.ap, R=[SS32], W=[SSbf])
                if d == 0:
                    K.dma("sp", YFS[tk:tk + 128, :], ysd.ap, R=[ysd], W=[YFS])
                    continue
                K.dma("sp", yf2.ap, YFS[tk:tk + 128, :], R=[YFS], W=[yf2])
                K.dma("sp", zsT.ap, PT_r[ROW_Z:ROW_Z + 1024, tk:tk + 128].rearrange("(cc p) t -> p cc t", p=128), R=[PT], W=[zsT])
                pss = [K.ps(), K.ps()]
                for cc in range(8):
                    K.tr(pss[cc // 4][:, (cc % 4) * 128:(cc % 4 + 1) * 128], zsT[:, cc, :], identf.ap, R=[zsT, identf], W=[pss[cc // 4]])
                for hb in range(2):
                    K.cp("act", zst[:, hb * 512:(hb + 1) * 512], pss[hb].ap, R=[pss[hb]], W=[zst])
                K.tt("pool", ysd.ap, ysd.ap, yf2.ap, ALU.add, R=[ysd, yf2], W=[ysd])
                K.tt("dve", yf2.ap.rearrange("p (h n) -> p h n", h=16), x3, bc(dsk.ap.unsqueeze(2), [128, 16, 64]), ALU.mult, R=[xst, dsk], W=[yf2])
                K.tt("pool", ysd.ap, ysd.ap, yf2.ap, ALU.add, R=[ysd, yf2], W=[ysd])
                K.tt("dve", ysd.ap, ysd.ap, zst.ap, ALU.mult, R=[ysd, zst], W=[ysd])
                K.act(yf2.ap, ysd.ap, AF.Square, accum=ss1[:, 0:1], R=[ysd], W=[yf2, ss1])
                K.rsqrt(ss1[:, 1:2], ss1[:, 0:1], 1.0 / 1024, 1e-5, R=[ss1], W=[ss1])
                K.stt(ob2.ap, ysd.ap, ss1[:, 1:2], nwb.ap, ALU.mult, ALU.mult, R=[ysd, ss1, nwb], W=[ob2])
                psT = K.ps()
                psTb = psT.ap.bitcast(BF16)
                for cc in range(8):
                    K.tr(psTb[:, cc * 128:(cc + 1) * 128], ob2[:, cc * 128:(cc + 1) * 128], identb.ap, R=[ob2, identb], W=[psT])
                K.cp("act", mixo2.ap, psTb.rearrange("p (a b) -> p a b", a=8), R=[psT], W=[mixo2])
                K.dma("sp", MIXT[1024:2048, tk:tk + 128].rearrange("(cc p) t -> p cc t", p=128), mixo2.ap, R=[mixo2], W=[MIXT])

        if stop == "C":
            return fin()
        K.reset()
        NT = min(512, HT)
        NB = NT // 128
        wout = K.carve("wout", [128, 16, D], BF16)

        woutb = [p.buf(f"woutb{i}") for i in range(4)]
        for q in range(4):
            p.op("sp", (lambda o_, i_: lambda e: e.dma_start(out=o_, in_=i_))(wout[:, :, q * 512:(q + 1) * 512], WOUTT[:, :, q * 512:(q + 1) * 512]),
                 reads=[WOUTT.b], writes=[woutb[q]], dma=True)
        xT = K.carve("xT", [128, 16, NT], F32)
        mixt = K.carve("mixt", [128, 16, NT], BF16)
        h2 = K.carve("h2", [128, 16, NT], BF16)
        xbl = [K.carve(f"xbl{i}", [128, D], F32) for i in range(2)]
        sqs = [K.carve(f"sq{i}", [128, NT], F32) for i in range(2)]
        rst = K.carve("rst", [128, NT], F32)

        def norm_stats(src, dstr):
            psn = K.ps()
            for k in range(16):
                sq = sqs[k % 2]
                K.tt("pool", sq.ap, src[:, k, :], src[:, k, :], ALU.mult, R=[src], W=[sq])
                K.mm(psn[:, 0:NT], onesf.ap, sq.ap, start=(k == 0), stop=(k == 15), R=[onesf, sq], W=[psn])
            K.rsqrt(dstr.ap, psn[:, 0:NT], 1.0 / D, 1e-6, R=[psn], W=[dstr])

        def load_xT(dst, t0):
            for b in range(NB):
                xb = xbl[b % 2]
                K.dma("sp", xb.ap, I["x"][t0 + b * 128:t0 + (b + 1) * 128, :], W=[xb])
                for q in range(4):
                    ps = K.ps()
                    for kk in range(4):
                        K.tr(ps[:, kk * 128:(kk + 1) * 128], xb[:, (q * 4 + kk) * 128:(q * 4 + kk + 1) * 128], identf.ap, R=[xb, identf], W=[ps])
                    K.cp("act" if q % 2 else "dve", dst[:, q * 4:(q + 1) * 4, b * 128:(b + 1) * 128], ps.ap.rearrange("p (a b) -> p a b", a=4), R=[ps], W=[dst])

        for ti in range(TT // NT):
            t0 = ti * NT
            hlf = t0 // HT
            load_xT(xT, t0)
            K.dma("sp", mixt.ap, MIXT[:, t0:t0 + NT].rearrange("(kc p) t -> p kc t", p=128), R=[MIXT], W=[mixt])
            for m in range(16):
                ps = K.ps()
                for k in range(16):
                    p.op("pe", (lambda o_, l_, r_, s0, s1_: lambda e: e.matmul(o_, lhsT=l_, rhs=r_, start=s0, stop=s1_))(
                        ps[:, 0:NT], wout[:, k, m * 128:(m + 1) * 128], mixt[:, k, :], k == 0, k == 15),
                        reads=[woutb[m // 4], mixt.b], writes=[ps.b])
                K.stt(xT[:, m, :], ps[:, 0:NT], modT[:, hlf, 2, m:m + 1], xT[:, m, :], ALU.mult, ALU.add, R=[ps, modT, xT], W=[xT])
            K.dma("sp", X1T[:, t0:t0 + NT].rearrange("(kc p) t -> p kc t", p=128), xT.ap, R=[xT], W=[X1T])
            norm_stats(xT, rst)
            for k in range(16):
                sq = sqs[k % 2]
                K.tt("dve", sq.ap, xT[:, k, :], rst.ap, ALU.mult, R=[xT, rst], W=[sq])
                K.act(h2[:, k, :], sq.ap, AF.Identity, bias=aff[:, 0, hlf, 3, k:k + 1], scale=aff[:, 0, hlf, 2, k:k + 1], R=[sq, aff], W=[h2])
            K.dma("sp", H2T[:, t0:t0 + NT].rearrange("(kc p) t -> p kc t", p=128), h2.ap, R=[h2], W=[H2T])

        if stop == "D1":
            return fin()
        K.reset()
        cf = K.carve("cf", [128, 88, 4], F32)
        fcw = I["ffn_conv_w"][0]
        for j in range(3):
            colvec(cf[:, :, j], fcw[j].rearrange("(c p) -> c p", p=128), 88, cf)
        colvec(cf[:, :, 3], I["ffn_conv_b"][0].rearrange("(c p) -> c p", p=128), 88, cf)
        h2t = K.carve("h2t", [128, 16, NT + 2], BF16)
        x1t = K.carve("x1t", [128, 16, NT], F32)
        aT = K.carve("aT", [128, 44, NT], BF16)
        wus = [K.carve(f"wu{i}", [128, 2, 16, 128], BF16) for i in range(4)]
        wds = [K.carve(f"wd{i}", [128, 44, 128], BF16) for i in range(4)]
        NS2 = NT // 256
        raw2 = [[K.carve(f"rawf{i}_{q}", [128, 258], F32) for q in range(2)] for i in range(2)]
        u2 = [[K.carve(f"u2_{i}_{q}", [128, 256], F32) for q in range(2)] for i in range(2)]
        s2b = [K.carve(f"s2b{i}", [128, 256], F32) for i in range(2)]
        sqs = [K.carve(f"sqd{i}", [128, NT], F32) for i in range(2)]
        rst = K.carve("rstd2", [128, NT], F32)
        yob = [K.carve(f"yob{i}", [128, D], F32) for i in range(1)]

        H2T_r = H2T.ap.rearrange("(kc p) t -> p kc t", p=128)
        ec = 0

        def ld_up(j):
            K.dma("act", wus[j % 4].ap, WUPT[j], R=[WUPT], W=[wus[j % 4]])

        def ld_dn(m):
            K.dma("act", wds[m % 4].ap, WDNT[m], R=[WDNT], W=[wds[m % 4]])

        for ti in range(TT // NT):
            t0 = ti * NT
            hlf = t0 // HT
            K.dma("sp", h2t[:, :, 1:NT + 1], H2T_r[:, :, t0:t0 + NT], R=[H2T], W=[h2t])
            if t0 == 0:
                K.memset("pool", h2t[:, :, 0:1], 0.0, W=[h2t])
            else:
                K.dma("sp", h2t[:, :, 0:1], H2T_r[:, :, t0 - 1:t0], R=[H2T], W=[h2t], slow=True)
                if t0 % HT == 0:
                    K.ts("dve", h2t[:, :, 0:1], h2t[:, :, 0:1], flag[:, 0:1], None, ALU.mult, R=[h2t, flag], W=[h2t])
            if t0 + NT == TT:
                K.memset("pool", h2t[:, :, NT + 1:NT + 2], 0.0, W=[h2t])
            else:
                K.dma("sp", h2t[:, :, NT + 1:NT + 2], H2T_r[:, :, t0 + NT:t0 + NT + 1], R=[H2T], W=[h2t], slow=True)
                if (t0 + NT) % HT == 0:
                    K.ts("dve", h2t[:, :, NT + 1:NT + 2], h2t[:, :, NT + 1:NT + 2], flag[:, 0:1], None, ALU.mult, R=[h2t, flag], W=[h2t])
            K.dma("sp", x1t.ap, X1T[:, t0:t0 + NT].rearrange("(kc p) t -> p kc t", p=128), R=[X1T], W=[x1t])
            if ti == 0:
                ld_up(0); ld_up(1); ld_up(2)
            for j in range(44):
                wu = wus[j % 4]
                jl = 0
                if j + 3 < 44:
                    ld_up(j + 3)
                else:
                    ld_dn(j + 3 - 44)
                for sub in range(NS2):
                    i = ec % 2
                    ec += 1
                    rw, uu, sb_ = raw2[i], u2[i], s2b[i]
                    pp = []
                    for part in range(2):
                        ps = K.ps()
                        pp.append(ps)
                        for k in range(16):
                            K.mm(ps[:, 0:258], wu[:, part, k, jl * 128:(jl + 1) * 128], h2t[:, k, sub * 256:sub * 256 + 258], start=(k == 0), stop=(k == 15), R=[wu, h2t], W=[ps])
                    for part in range(2):
                        K.cp("act", rw[part].ap, pp[part][:, 0:258], R=[pp[part]], W=[rw[part]])
                    cchs = [part * 44 + j for part in range(2)]
                    for part in range(2):
                        K.ts("dve", uu[part].ap, rw[part][:, 0:256], cf[:, cchs[part], 0:1], cf[:, cchs[part], 3:4], ALU.mult, ALU.add, R=[rw[part], cf], W=[uu[part]])
                    for tp in (1, 2):
                        for part in range(2):
                            K.stt(uu[part].ap, rw[part][:, tp:tp + 256], cf[:, cchs[part], tp:tp + 1], uu[part].ap, ALU.mult, ALU.add, R=[rw[part], cf, uu[part]], W=[uu[part]])
                    K.act(sb_.ap, uu[0].ap, AF.Silu, R=[uu[0]], W=[sb_])
                    K.tt("pool", aT[:, j, sub * 256:(sub + 1) * 256], sb_.ap, uu[1].ap, ALU.mult, R=[sb_, uu[1]], W=[aT])
            for m in range(16):
                wd_ = wds[m % 4]
                ml = 0
                if m + 3 < 16:
                    ld_dn(m + 3)
                elif ti + 1 < TT // NT:
                    ld_up(m + 3 - 16)
                ps = K.ps()
                for j in range(44):
                    K.mm(ps[:, 0:NT], wd_[:, j, ml * 128:(ml + 1) * 128], aT[:, j, :], start=(j == 0), stop=(j == 43), R=[wd_, aT], W=[ps])
                K.stt(x1t[:, m, :], ps[:, 0:NT], modT[:, hlf, 5, m:m + 1], x1t[:, m, :], ALU.mult, ALU.add, R=[ps, modT, x1t], W=[x1t])
            norm_stats(x1t, rst)
            for k in range(16):
                K.tt("dve", x1t[:, k, :], x1t[:, k, :], rst.ap, ALU.mult, R=[x1t, rst], W=[x1t])
                K.act(x1t[:, k, :], x1t[:, k, :], AF.Identity, bias=aff[:, 0, hlf, 5, k:k + 1], scale=aff[:, 0, hlf, 4, k:k + 1], R=[x1t, aff], W=[x1t])
            for b in range(NB):
                yo_ = yob[0]
                for q in range(4):
                    ps = K.ps()
                    for kk in range(4):
                        k = q * 4 + kk
                        K.tr(ps[:, kk * 128:(kk + 1) * 128], x1t[:, k, b * 128:(b + 1) * 128], identf.ap, R=[x1t, identf], W=[ps])
                    K.cp("act" if q % 2 else "dve", yo_[:, q * 512:(q + 1) * 512], ps.ap, R=[ps], W=[yo_])
                p.op("sp", (lambda o_, i_: lambda e: e.dma_start(out=o_, in_=i_))(y_out[t0 + b * 128:t0 + (b + 1) * 128, :], yo_.ap),
                     reads=[yo_.b], writes=[OUTB], dma=True)
        p.barrier()
        p.run(engsems, dmasems, block)
    return nc


_CACHE = {}


def _get_nc(TT):
    if TT not in _CACHE:
        _CACHE[TT] = build_program(TT)
    return _CACHE[TT]


WEIGHT_NAMES = ["norm1_w", "w_in", "rwkv_mu", "rwkv_w0", "rwkv_w2", "rwkv_a0", "rwkv_a2", "rwkv_g2", "rwkv_k_k", "rwkv_k_a",
                "rwkv_r_k", "rwkv_lnx_w", "rwkv_lnx_b", "ssm_conv_w", "ssm_conv_b", "ssm_dt_bias", "ssm_a_log", "ssm_d",
                "ssm_norm_w", "w_out", "norm2_w", "ffn_w_up", "ffn_conv_w", "ffn_conv_b", "ffn_w_down", "w_ada", "b_ada",
                "final_norm_w", "w_ada_final", "b_ada_final"]


def kernel(**inputs):
    xp = np.ascontiguousarray(inputs["x_prompt"], dtype=np.float32)
    xs = np.ascontiguousarray(inputs["x_sample"], dtype=np.float32)
    cp_ = np.asarray(inputs["c_prompt"], dtype=np.float32)
    cs_ = np.asarray(inputs["c_sample"], dtype=np.float32)
    Bp, Tp, _ = xp.shape
    Bs, Ts, _ = xs.shape
    TT = 2 * Tp
    assert Ts == TT and Bp == 2 * Bs and Bp // 2 + Bs == 8
    w = {k: np.ascontiguousarray(inputs[k], dtype=np.float32) for k in WEIGHT_NAMES}
    in_maps = []
    for c in range(8):
        m = dict(w)
        if c < Bp // 2:
            m["x"] = xp[2 * c:2 * c + 2].reshape(TT, D)
            m["c2"] = cp_[2 * c:2 * c + 2]
            m["flag"] = np.zeros((128, 1), np.float32)
        else:
            s = c - Bp // 2
            m["x"] = xs[s]
            m["c2"] = np.stack([cs_[s], cs_[s]])
            m["flag"] = np.ones((128, 1), np.float32)
        in_maps.append(m)
    nc = _get_nc(TT)
    res = run_bass_kernel_spmd(nc, in_maps, core_ids=list(range(8)))
    outs = [np.asarray(r["y"], dtype=np.float32) for r in res.results]
    y_prompt = np.stack([o.reshape(2, Tp, D) for o in outs[:Bp // 2]]).reshape(Bp, Tp, D)
    y_sample = np.stack(outs[Bp // 2:]).reshape(Bs, Ts, D)
    return (y_prompt, y_sample)
```
